# Optimizing a Trainium2 kernel written in Bass

```python
import jax, jax.numpy as jnp
from jax import lax
import numpy as np

D_MODEL = 1024
BATCH = 32
SEQ = 2048
DEPTH = 1
DEC_BATCH = 8
DEC_SEQ = 4096
PAST_LEN = 128

M_HEADS = 4
M_HEAD_DIM = 128
M_WIDTH = M_HEADS * M_HEAD_DIM
M_CHUNK = 64
N_GATE = 4 * M_HEADS
A_HEADS = 8
A_KV_HEADS = 2
A_GROUP = A_HEADS // A_KV_HEADS
A_HEAD_DIM = 64
A_WIDTH = A_HEADS * A_HEAD_DIM
A_KV_WIDTH = A_KV_HEADS * A_HEAD_DIM
WINDOW = 128
A_BLOCK = WINDOW
ROPE_THETA = 10000.0
D_FF = 2816
IN_COLS = 4 * M_WIDTH + N_GATE + A_WIDTH + 2 * A_KV_WIDTH + 2 * D_MODEL
EPS = 1e-6
NEG = -1e30

kernel_name = 'hybrid_mlstm_swa_encoder'


def rms_norm(x, g):
    xf = x.astype(jnp.float32)
    xf = xf * lax.rsqrt(jnp.mean(xf * xf, axis=-1, keepdims=True) + EPS)
    return xf.astype(x.dtype) * g


def swiglu(x, w1, w3, w2):
    return (jax.nn.silu(x @ w1) * (x @ w3)) @ w2


def rope(x, pos):
    half = x.shape[-1] // 2
    inv = jnp.power(ROPE_THETA, -jnp.arange(half, dtype=jnp.float32) / half)
    ang = pos.astype(jnp.float32)[:, None] * inv[None, :]
    cos = jnp.cos(ang)[:, None, :]
    sin = jnp.sin(ang)[:, None, :]
    xf = x.astype(jnp.float32)
    x1, x2 = xf[..., :half], xf[..., half:]
    return jnp.concatenate([x1 * cos - x2 * sin, x2 * cos + x1 * sin], axis=-1).astype(x.dtype)


def mlstm_dir(q, k, v, log_i, log_f):
    B, H, S, Dh = q.shape
    L = M_CHUNK
    NC = S // L
    qc = q.reshape(B, H, NC, L, Dh)
    kc = k.reshape(B, H, NC, L, Dh)
    vc = v.reshape(B, H, NC, L, Dh)
    li = log_i.reshape(B, H, NC, L)
    bcum = jnp.cumsum(log_f.reshape(B, H, NC, L), axis=-1)
    g = bcum[..., -1]
    a = g[..., None] - bcum + li
    m_loc = jnp.max(a, axis=-1)
    w = jnp.exp(a - m_loc[..., None])
    dC = jnp.einsum('bhcsk,bhcsv->bhckv', kc * w[..., None], vc)
    dn = jnp.einsum('bhcsk,bhcs->bhck', kc, w)

    def step(carry, inp):
        C, n, m = carry
        dC_c, dn_c, g_c, ml_c = inp
        m_new = jnp.maximum(g_c + m, ml_c)
        s_old = jnp.exp(g_c + m - m_new)
        s_new = jnp.exp(ml_c - m_new)
        C_new = s_old[..., None, None] * C + s_new[..., None, None] * dC_c
        n_new = s_old[..., None] * n + s_new[..., None] * dn_c
        return (C_new, n_new, m_new), (C, n, m)

    init = (jnp.zeros((B, H, Dh, Dh), jnp.float32),
            jnp.zeros((B, H, Dh), jnp.float32),
            jnp.full((B, H), NEG, jnp.float32))
    xs = (jnp.moveaxis(dC, 2, 0), jnp.moveaxis(dn, 2, 0),
          jnp.moveaxis(g, 2, 0), jnp.moveaxis(m_loc, 2, 0))
    _, (Cp, npv, mp) = lax.scan(step, init, xs)
    Cp = jnp.moveaxis(Cp, 0, 2)
    npv = jnp.moveaxis(npv, 0, 2)
    mp = jnp.moveaxis(mp, 0, 2)

    causal = jnp.tril(jnp.ones((L, L), dtype=bool))
    dmat = bcum[..., :, None] - bcum[..., None, :] + li[..., None, :]
    dmat = jnp.where(causal, dmat, NEG)
    m_inter = bcum + mp[..., None]
    m_t = jnp.maximum(jnp.max(dmat, axis=-1), m_inter)
    p = jnp.exp(dmat - m_t[..., None])
    wq = p * jnp.einsum('bhctd,bhcsd->bhcts', qc, kc)
    s_inter = jnp.exp(m_inter - m_t)
    num = (jnp.einsum('bhcts,bhcsv->bhctv', wq, vc)
           + s_inter[..., None] * jnp.einsum('bhctk,bhckv->bhctv', qc, Cp))
    den = jnp.sum(wq, axis=-1) + s_inter * jnp.einsum('bhctk,bhck->bhct', qc, npv)
    h = num / jnp.maximum(jnp.abs(den), jnp.exp(-m_t))[..., None]
    return h.reshape(B, H, S, Dh)


def mlstm_branch(mq, mk, mv, mo, mg, b_gates, m_norm):
    B, S, _ = mq.shape

    def heads(t):
        return t.reshape(B, S, M_HEADS, M_HEAD_DIM).transpose(0, 2, 1, 3).astype(jnp.float32)

    qh = heads(mq)
    kh = heads(mk) * (M_HEAD_DIM ** -0.5)
    vh = heads(mv)
    gt = (mg + b_gates).astype(jnp.float32).reshape(B, S, 4, M_HEADS).transpose(2, 0, 3, 1)
    i_f, f_f, i_b, f_b = gt[0], gt[1], gt[2], gt[3]
    h_fwd = mlstm_dir(qh, kh, vh, i_f, jax.nn.log_sigmoid(f_f))
    flip = lambda t: jnp.flip(t, axis=2)
    h_bwd = flip(mlstm_dir(flip(qh), flip(kh), flip(vh), flip(i_b), flip(jax.nn.log_sigmoid(f_b))))
    h = h_fwd + h_bwd
    h = h * lax.rsqrt(jnp.mean(h * h, axis=-1, keepdims=True) + EPS)
    h = h.transpose(0, 2, 1, 3).reshape(B, S, M_WIDTH).astype(mq.dtype) * m_norm
    return h * jax.nn.sigmoid(mo)


def window_attn_branch(aq, ak, av, q_norm, k_norm, sink):
    B, S, _ = aq.shape
    NB = S // A_BLOCK
    q = rms_norm(aq.reshape(B, S, A_HEADS, A_HEAD_DIM), q_norm)
    k = rms_norm(ak.reshape(B, S, A_KV_HEADS, A_HEAD_DIM), k_norm)
    v = av.reshape(B, S, A_KV_HEADS, A_HEAD_DIM)
    pos = jnp.arange(S)
    q = rope(q, pos)
    k = rope(k, pos)

    def band(t):
        tp = jnp.pad(t, ((0, 0), (WINDOW, WINDOW), (0, 0), (0, 0)))
        tp = tp.reshape(B, NB + 2, A_BLOCK, A_KV_HEADS, A_HEAD_DIM)
        return jnp.concatenate([tp[:, :-2], tp[:, 1:-1], tp[:, 2:]], axis=2)

    kw = band(k)
    vw = band(v)
    qb = q.reshape(B, NB, A_BLOCK, A_KV_HEADS, A_GROUP, A_HEAD_DIM)
    s = jnp.einsum('bnqhgd,bnkhd->bnhgqk', qb, kw).astype(jnp.float32) * (A_HEAD_DIM ** -0.5)
    qi = jnp.arange(A_BLOCK)
    kj = jnp.arange(3 * A_BLOCK)
    rel = kj[None, :] - WINDOW - qi[:, None]
    kpos = jnp.arange(NB)[:, None] * A_BLOCK - WINDOW + kj[None, :]
    mask = (jnp.abs(rel) <= WINDOW)[None, :, :] & ((kpos >= 0) & (kpos < S))[:, None, :]
    s = jnp.where(mask[None, :, None, None, :, :], s, NEG)
    sk = sink.astype(jnp.float32).reshape(A_KV_HEADS, A_GROUP)[None, None, :, :, None, None]
    m = jnp.maximum(jnp.max(s, axis=-1, keepdims=True), sk)
    p = jnp.exp(s - m)
    den = jnp.sum(p, axis=-1, keepdims=True) + jnp.exp(sk - m)
    o = jnp.einsum('bnhgqk,bnkhd->bnqhgd', (p / den).astype(v.dtype), vw)
    return o.reshape(B, S, A_WIDTH)


def encoder_layer(x, ffn1_norm, ffn1_w1, ffn1_w3, ffn1_w2, mix_norm, w_in, b_gates, m_norm,
                  q_norm, k_norm, sink, w_pm, w_pa, w_out, ffn2_norm, ffn2_w1, ffn2_w3, ffn2_w2):
    x = x + 0.5 * swiglu(rms_norm(x, ffn1_norm), ffn1_w1, ffn1_w3, ffn1_w2)
    h = rms_norm(x, mix_norm)
    z = h @ w_in
    sizes = (M_WIDTH, M_WIDTH, M_WIDTH, M_WIDTH, N_GATE, A_WIDTH, A_KV_WIDTH, A_KV_WIDTH, D_MODEL, D_MODEL)
    idx = [sum(sizes[:i + 1]) for i in range(len(sizes) - 1)]
    mq, mk, mv, mo, mg, aq, ak, av, gm, ga = jnp.split(z, idx, axis=-1)
    hm = mlstm_branch(mq, mk, mv, mo, mg, b_gates, m_norm)
    ha = window_attn_branch(aq, ak, av, q_norm, k_norm, sink)
    merged = jax.nn.sigmoid(gm) * (hm @ w_pm) + jax.nn.sigmoid(ga) * (ha @ w_pa)
    x = x + merged @ w_out
    x = x + 0.5 * swiglu(rms_norm(x, ffn2_norm), ffn2_w1, ffn2_w3, ffn2_w2)
    return x


def setup_inputs(seed: int = 0) -> dict:
    key = jax.random.key(seed)
    ks = jax.random.split(key, 24)
    f32 = jnp.float32

    def nrm(k, shape, fan_in):
        return jax.random.normal(k, shape, f32) * (fan_in ** -0.5)

    def gain(k, shape):
        return 1.0 + 0.05 * jax.random.normal(k, shape, f32)

    gate_base = jnp.tile(jnp.concatenate([jnp.zeros((M_HEADS,), f32),
                                          jnp.linspace(3.0, 6.0, M_HEADS, dtype=f32)]), 2)
    return {
        'x_prompt': jax.random.normal(ks[0], (BATCH, SEQ, D_MODEL), f32),
        'x_sample': jax.random.normal(ks[1], (DEC_BATCH, DEC_SEQ, D_MODEL), f32),
        'ffn1_norm': gain(ks[2], (DEPTH, D_MODEL)),
        'ffn1_w1': nrm(ks[3], (DEPTH, D_MODEL, D_FF), D_MODEL),
        'ffn1_w3': nrm(ks[4], (DEPTH, D_MODEL, D_FF), D_MODEL),
        'ffn1_w2': nrm(ks[5], (DEPTH, D_FF, D_MODEL), D_FF),
        'mix_norm': gain(ks[6], (DEPTH, D_MODEL)),
        'w_in': nrm(ks[7], (DEPTH, D_MODEL, IN_COLS), D_MODEL),
        'b_gates': gate_base[None, :] + 0.1 * jax.random.normal(ks[8], (DEPTH, N_GATE), f32),
        'm_norm': gain(ks[9], (DEPTH, M_WIDTH)),
        'q_norm': gain(ks[10], (DEPTH, A_HEAD_DIM)),
        'k_norm': gain(ks[11], (DEPTH, A_HEAD_DIM)),
        'sink': 0.5 * jax.random.normal(ks[12], (DEPTH, A_HEADS), f32),
        'w_pm': nrm(ks[13], (DEPTH, M_WIDTH, D_MODEL), M_WIDTH),
        'w_pa': nrm(ks[14], (DEPTH, A_WIDTH, D_MODEL), A_WIDTH),
        'w_out': nrm(ks[15], (DEPTH, D_MODEL, D_MODEL), D_MODEL),
        'ffn2_norm': gain(ks[16], (DEPTH, D_MODEL)),
        'ffn2_w1': nrm(ks[17], (DEPTH, D_MODEL, D_FF), D_MODEL),
        'ffn2_w3': nrm(ks[18], (DEPTH, D_MODEL, D_FF), D_MODEL),
        'ffn2_w2': nrm(ks[19], (DEPTH, D_FF, D_MODEL), D_FF),
    }


def reference(x_prompt, x_sample, ffn1_norm, ffn1_w1, ffn1_w3, ffn1_w2, mix_norm, w_in, b_gates,
              m_norm, q_norm, k_norm, sink, w_pm, w_pa, w_out, ffn2_norm, ffn2_w1, ffn2_w3, ffn2_w2):
    y_prompt = x_prompt
    y_sample = x_sample
    for l in range(DEPTH):
        y_prompt = encoder_layer(y_prompt, ffn1_norm[l], ffn1_w1[l], ffn1_w3[l], ffn1_w2[l], mix_norm[l],
                                 w_in[l], b_gates[l], m_norm[l], q_norm[l], k_norm[l], sink[l], w_pm[l],
                                 w_pa[l], w_out[l], ffn2_norm[l], ffn2_w1[l], ffn2_w3[l], ffn2_w2[l])
        y_sample = encoder_layer(y_sample, ffn1_norm[l], ffn1_w1[l], ffn1_w3[l], ffn1_w2[l], mix_norm[l],
                                 w_in[l], b_gates[l], m_norm[l], q_norm[l], k_norm[l], sink[l], w_pm[l],
                                 w_pa[l], w_out[l], ffn2_norm[l], ffn2_w1[l], ffn2_w3[l], ffn2_w2[l])
    return (y_prompt, y_sample)
```

```python
import math
import types
from contextlib import ExitStack
import numpy as np
import concourse.bass as bass
import concourse.mybir as mybir
from concourse.bass_utils import run_bass_kernel_spmd

F32 = mybir.dt.float32
BF16 = mybir.dt.bfloat16
AF = mybir.ActivationFunctionType
ALU = mybir.AluOpType
AX = mybir.AxisListType

D = 1024
DFF = 2816
NJ = 22
EPS = 1e-6
NG = 4
T = NG * 128
N_CORES = 8
PROMPT_PER_CORE = 4
SEQ = 2048
DEC_SEQ = 4096
NBMAX = 32
C_MQ, C_MK, C_MV, C_MO, C_MG, C_AQ, C_AK, C_AV, C_GM, C_GA = 0, 512, 1024, 1536, 2048, 2064, 2576, 2704, 2832, 3856
U_F1 = 0
U_F1W2 = 11
U_QT = 17
U_K = 18
U_GM = 19
U_GA = 21
U_V = 23
U_MO = 24
U_AQ = 25
U_KVG = 26
U_PM = 27
U_PA = 28
U_WO = 29
U_F2 = 31
U_F2W2 = 42
NUNITS = 48
K_ID, K_LE, K_GE, K_BLK, K_MF, K_MB, K_ONE, K_S0, K_S1 = 0, 128, 256, 384, 512, 640, 768, 896, 1024
NCST = 1152


def _freeze(fn):
    if fn.__closure__ is None:
        return fn
    cells = []
    for c in fn.__closure__:
        try:
            cells.append(types.CellType(c.cell_contents))
        except ValueError:
            cells.append(c)
    return types.FunctionType(fn.__code__, fn.__globals__, fn.__name__, fn.__defaults__, tuple(cells))


class Buf:
    __slots__ = ("w", "r", "excl")

    def __init__(self, excl=False):
        self.w = None
        self.r = {}
        self.excl = excl


class FW:
    def __init__(self, nc, stack):
        self.nc = nc
        self.stack = stack
        self.names = ['pe', 'act', 'dve', 'pool', 'sp']
        self.sem = {}
        self.cnt = {}
        self.known = {k: {} for k in self.names}
        self.nsem = 0
        self._new_sems()
        self.NDS = 8
        self.dsem = {}
        self.dcnt = {}
        self.dnext = {}
        for q in ('sp', 'pool'):
            self.dsem[q] = [stack.enter_context(nc.semaphore('d_%s%d' % (q, i))) for i in range(self.NDS)]
            self.dcnt[q] = [0] * self.NDS
            self.dnext[q] = 0
        self.ops = {k: [] for k in self.names}
        self.nops = 0
        self.maxops = 10 ** 9

    def _new_sems(self):
        for k in self.names:
            self.sem[k] = self.stack.enter_context(self.nc.semaphore('s_%s%d' % (k, self.nsem)))
            self.cnt[k] = 0
        self.nsem += 1

    def _wait(self, e, tok):
        sem, val, src = tok
        if src == e and e == 'pe':
            return
        kn = self.known[e]
        key = id(sem)
        if kn.get(key, 0) >= val:
            return
        kn[key] = val
        self.ops[e].append(('wait', sem, val))

    def _deps(self, e, reads, writes):
        for b in reads:
            if b.w is not None:
                self._wait(e, b.w)
            if b.excl:
                for t in b.r.values():
                    if t[2] != e:
                        self._wait(e, t)
        for b in writes:
            if b.w is not None and b.w[2] != e:
                self._wait(e, b.w)
            for t in b.r.values():
                if t[2] != e:
                    self._wait(e, t)

    def _mark(self, tok, reads, writes):
        key = id(tok[0])
        for b in reads:
            b.r[key] = tok
        for b in writes:
            b.w = tok
            b.r = {}

    def op(self, e, fn, reads=(), writes=()):
        self.nops += 1
        if self.nops > self.maxops:
            return None
        self._deps(e, reads, writes)
        self.cnt[e] += 1
        tok = (self.sem[e], self.cnt[e], e)
        self.ops[e].append(('op', _freeze(fn), self.sem[e]))
        self._mark(tok, reads, writes)
        return tok

    def dma(self, q, out, in_, reads=(), writes=()):
        self.nops += 1
        if self.nops > self.maxops:
            return None
        self._deps(q, reads, writes)
        j = self.dnext[q]
        self.dnext[q] = (j + 1) % self.NDS
        sem = self.dsem[q][j]
        if self.dcnt[q][j] > 0:
            self._wait(q, (sem, self.dcnt[q][j], 'dma'))
        self.dcnt[q][j] += 16
        tok = (sem, self.dcnt[q][j], 'dma')
        self.ops[q].append(('dma', out, in_, sem))
        self._mark(tok, reads, writes)
        return tok

    def barrier(self):
        toks = [(self.sem[k], self.cnt[k], k) for k in self.names if self.cnt[k] > 0]
        for q in self.dsem:
            for j in range(self.NDS):
                if self.dcnt[q][j] > 0:
                    toks.append((self.dsem[q][j], self.dcnt[q][j], 'dma'))
        for e in self.names:
            for t in toks:
                if t[2] != e:
                    self._wait(e, t)
        self._new_sems()

    def emit(self):
        nc = self.nc
        ops = self.ops
        with nc.Block() as block:
            def mk(e):
                def body(engine):
                    pend = []
                    for o in ops[e]:
                        if o[0] == 'wait':
                            pend.append(o)
                            continue
                        for w in pend[:-1]:
                            engine.wait_ge(w[1], w[2])
                        if o[0] == 'op':
                            ins = o[1](engine)
                        else:
                            ins = engine.dma_start(out=o[1], in_=o[2])
                        if pend:
                            ins._wait_ge(pend[-1][1], pend[-1][2])
                        ins.then_inc(o[2] if o[0] == 'op' else o[3], 1 if o[0] == 'op' else 16)
                        pend = []
                    for w in pend:
                        engine.wait_ge(w[1], w[2])
                return body
            block.tensor(mk('pe'))
            block.scalar(mk('act'))
            block.vector(mk('dve'))
            block.gpsimd(mk('pool'))
            block.sync(mk('sp'))


def build(seqs, debug=False, stage=3, maxops=10 ** 9):
    NTOK = sum(seqs)
    NBLK = NTOK // 128
    NTILE = NTOK // T
    nc = bass.Bass("TRN2", target_bir_lowering=False)
    IN = "ExternalInput"
    SCR = "ExternalOutput" if debug else "Internal"

    def din(name, shape, dt=F32):
        return nc.dram_tensor(name, shape, dt, kind=IN).ap()

    def dscr(name, shape, dt):
        return nc.dram_tensor(name, shape, dt, kind=SCR).ap()

    x_d = din("x", [NTOK, D])
    f1w1 = din("f1w1", [D, DFF]); f1w3 = din("f1w3", [D, DFF]); f1w2 = din("f1w2", [DFF, D])
    f2w1 = din("f2w1", [D, DFF]); f2w3 = din("f2w3", [D, DFF]); f2w2 = din("f2w2", [DFF, D])
    win = din("win", [D, 4880]); wpm = din("wpm", [512, D]); wpa = din("wpa", [512, D]); wout = din("wout", [D, D])
    gcol_d = din("gcol", [128, 28])
    vec_d = din("vec", [128, 160])
    cst_d = din("cst", [128, NCST])
    rope_d = din("rope", [128, NBMAX, 64])
    y_d = nc.dram_tensor("y", [NTOK, D], F32, kind="ExternalOutput").ap()

    WS = dscr("WS", [NUNITS, 128, 4096], BF16)
    X1 = dscr("X1", [NTOK, D], F32)
    ZQT = dscr("ZQT", [4, 128, NTOK], BF16)
    ZKT = dscr("ZKT", [4, 128, NTOK], BF16)
    ZK = dscr("ZK", [NTOK, 512], BF16)
    ZV = dscr("ZV", [NTOK, 512], BF16)
    TMO = dscr("TMO", [NTOK, 512], BF16)
    ZAQ = dscr("ZAQ", [NTOK, 512], BF16)
    ZAKV = dscr("ZAKV", [NTOK, 256], BF16)
    GT = dscr("GT", [128, NBLK, 16], F32)
    GMT = dscr("GMT", [8, 128, NTOK], BF16)
    GAT = dscr("GAT", [8, 128, NTOK], BF16)
    HMT = dscr("HMT", [4, 128, NTOK], BF16)
    HAT = dscr("HAT", [4, 128, NTOK], BF16)

    with ExitStack() as top:
        fw = FW(nc, top)
        fw.maxops = maxops
        op = fw.op

        uid = [0]

        def SB(st, name, shape, dt):
            uid[0] += 1
            return st.enter_context(nc.sbuf_tensor("sb%d_%s" % (uid[0], name), shape, dt))

        def PS(st, name, shape, dt):
            uid[0] += 1
            return st.enter_context(nc.psum_tensor("ps%d_%s" % (uid[0], name), shape, dt))

        cst = SB(top, "cst", [128, NCST], F32); b_cst = Buf()
        cstb = SB(top, "cstb", [128, NCST], BF16); b_cstb = Buf()
        gcol = SB(top, "gcol", [128, 28], F32); b_gcol = Buf()
        vecb = SB(top, "vecb", [128, 160], F32); b_vec = Buf()
        cm05 = SB(top, "cm05", [128, 64], F32); b_cm = Buf()
        fw.dma('sp', cst[:], cst_d[:, :], writes=[b_cst])
        fw.dma('sp', gcol[:], gcol_d[:, :], writes=[b_gcol])
        fw.dma('sp', vecb[:], vec_d[:, :], writes=[b_vec])
        op('dve', lambda e: e.tensor_copy(cstb[:], cst[:]), reads=[b_cst], writes=[b_cstb])
        op('dve', lambda e: e.memset(cm05[:], -0.5), writes=[b_cm])
        identb = cstb[:, K_ID:K_ID + 128]

        b_ws = [Buf() for _ in range(NUNITS)]
        cv = {'n': 0}
        udesc = {}
        for ffn_, (w1, w3, w2, ub, ub2, gb) in enumerate([(f1w1, f1w3, f1w2, U_F1, U_F1W2, 0), (f2w1, f2w3, f2w2, U_F2, U_F2W2, 16)]):
            for u in range(11):
                udesc[ub + u] = ([(w1[:, u * 256:(u + 1) * 256], 0, 256), (w3[:, u * 256:(u + 1) * 256], 256, 256)], 8, gb, 1.0, 'dve')
            for u in range(6):
                nj = 4 if u < 5 else 2
                udesc[ub2 + u] = ([(w2[u * 512:u * 512 + nj * 128, :], 0, 1024)], nj, None, 1.0, 'act')
        udesc[U_QT] = ([(win[:, C_MQ:C_MQ + 512], 0, 512)], 8, 8, 1.0, 'dve')
        udesc[U_K] = ([(win[:, C_MK:C_MK + 512], 0, 512)], 8, 8, 1.0, 'dve')
        for i_ in range(2):
            udesc[U_GM + i_] = ([(win[:, C_GM + i_ * 512:C_GM + (i_ + 1) * 512], 0, 512)], 8, 8, 1.0, 'dve')
            udesc[U_GA + i_] = ([(win[:, C_GA + i_ * 512:C_GA + (i_ + 1) * 512], 0, 512)], 8, 8, 1.0, 'dve')
        udesc[U_V] = ([(win[:, C_MV:C_MV + 512], 0, 512)], 8, 8, 1.0, 'dve')
        udesc[U_MO] = ([(win[:, C_MO:C_MO + 512], 0, 512)], 8, 8, 1.0, 'dve')
        udesc[U_AQ] = ([(win[:, C_AQ:C_AQ + 512], 0, 512)], 8, 8, 1.0, 'dve')
        udesc[U_KVG] = ([(win[:, C_AK:C_AK + 256], 0, 256), (win[:, C_MG:C_MG + 16], 256, 16)], 8, 8, 1.0, 'dve')
        udesc[U_PM] = ([(wpm[:, :], 0, 1024)], 4, 24, 0.5, 'dve')
        udesc[U_PA] = ([(wpa[:, :], 0, 1024)], 4, None, 1.0, 'act')
        for c in range(2):
            udesc[U_WO + c] = ([(wout[:, c * 512:(c + 1) * 512], 0, 512)], 8, None, 0.5, 'act')
        units_a = list(range(U_F1, U_F1 + 17)) + [U_QT, U_K, U_GM, U_GM + 1, U_GA, U_GA + 1, U_V, U_MO, U_AQ, U_KVG]
        units_c = [U_PM, U_PA, U_WO, U_WO + 1] + list(range(U_F2, U_F2 + 17))

        def conv_load(u):
            pieces, kcn, gbase, mult, eng = udesc[u]
            i = cv['n'] % 2
            cv['n'] += 1
            width = 512 if kcn == 8 else 1024
            s32 = cv['stg'][i][:, 0:kcn * width].rearrange("p (k c) -> p k c", k=kcn)
            pbufs = []
            for (src, off, w) in pieces:
                pb_ = Buf()
                fw.dma('sp', s32[:, :, off:off + w], src.rearrange("(k p) c -> p k c", p=128), writes=[pb_, cv['b_stg'][i]])
                pbufs.append(pb_)
            return (u, i, pbufs)

        def conv_finish(state):
            u, i, pbufs = state
            pieces, kcn, gbase, mult, eng = udesc[u]
            width = 512 if kcn == 8 else 1024
            s32 = cv['stg'][i][:, 0:kcn * width].rearrange("p (k c) -> p k c", k=kcn)
            s16 = cv['stb'][:, 0:kcn * width].rearrange("p (k c) -> p k c", k=kcn)
            b_stb = cv['b_stb']
            used = max(off + w for (_, off, w) in pieces)
            for kc in range(kcn):
                if gbase is None:
                    if eng == 'dve':
                        op('dve', lambda e, kc=kc: e.tensor_scalar(s16[:, kc, 0:used], s32[:, kc, 0:used], float(mult), None, ALU.mult),
                           reads=pbufs, writes=[b_stb])
                    else:
                        op('act', lambda e, kc=kc: e.activation(s16[:, kc, 0:used], s32[:, kc, 0:used], AF.Copy, scale=float(mult)),
                           reads=pbufs, writes=[b_stb])
                else:
                    gc = gcol[:, gbase + kc:gbase + kc + 1]
                    op('dve', lambda e, kc=kc, gc=gc: e.tensor_scalar(s16[:, kc, 0:used], s32[:, kc, 0:used], gc, float(mult), ALU.mult, ALU.mult),
                       reads=pbufs + [b_gcol], writes=[b_stb])
            fw.dma('pool', WS[u, :, 0:kcn * width], cv['stb'][:, 0:kcn * width], reads=[b_stb, cv['b_stg'][i]], writes=[b_ws[u]])

        def make_ffn_ctx(st, two_xnT=False):
            cx = {}
            cx['xres'] = SB(st, "xres", [128, NG, D], F32); cx['b_x'] = [Buf() for _ in range(NG)]
            cx['junk'] = SB(st, "junk", [128, D], BF16); cx['b_junk'] = Buf()
            cx['ss'] = SB(st, "ss", [128, 8], F32); cx['b_ss'] = [Buf() for _ in range(NG)]
            cx['xn'] = [SB(st, "xn%d" % i, [128, D], BF16) for i in range(2)]; cx['b_xn'] = [Buf(), Buf()]
            cx['xnT'] = SB(st, "xnT", [128, 8, T], BF16); cx['b_xnT'] = [Buf() for _ in range(NG)]
            if two_xnT:
                cx['xnT2'] = SB(st, "xnT2", [128, 8, T], BF16); cx['b_xnT2'] = [Buf() for _ in range(NG)]
            cx['hT'] = SB(st, "hT", [128, NJ, T], BF16); cx['b_hT'] = Buf()
            cx['w2'] = SB(st, "w2res", [128, 24, 1024], BF16); cx['b_w2'] = [Buf() for _ in range(6)]
            cx['ring'] = [SB(st, "ring%d" % i, [128, 4096], BF16) for i in range(3)]; cx['b_ring'] = [Buf() for _ in range(3)]
            cx['rn'] = 0
            cx['sg'] = [SB(st, "sg%d" % i, [128, T], BF16) for i in range(2)]; cx['b_sg'] = [Buf(), Buf()]
            cx['ptr'] = [PS(st, "ptr%d" % i, [128, 8, 128], BF16) for i in range(2)]; cx['b_ptr'] = [Buf(True), Buf(True)]
            cx['pacc'] = [PS(st, "pacc%d" % i, [128, 512], F32) for i in range(6)]; cx['b_pacc'] = [Buf(True) for _ in range(6)]
            cx['pn'] = 0
            cx['nn'] = 0
            return cx

        def ring_load(cx, u):
            i = cx['rn'] % 3
            cx['rn'] += 1
            fw.dma('sp', cx['ring'][i][:], WS[u, :, :], reads=[b_ws[u]], writes=[cx['b_ring'][i]])
            return cx['ring'][i], cx['b_ring'][i]

        def next_acc(cx):
            i = cx['pn'] % 6
            cx['pn'] += 1
            return cx['pacc'][i], cx['b_pacc'][i]

        def norm_front(cx, g):
            xres, ss = cx['xres'], cx['ss']
            i = cx['nn'] % 2
            cx['nn'] += 1
            bx, bss = cx['b_x'][g], cx['b_ss'][g]
            op('dve', lambda e: e.memset(ss[:, g:g + 1], 0.0), writes=[bss])
            op('act', lambda e: e.activation(cx['junk'][:], xres[:, g, :], AF.Square, accum_out=ss[:, g:g + 1]),
               reads=[bx, bss], writes=[cx['b_junk'], bss])
            op('dve', lambda e: e.tensor_scalar(ss[:, g:g + 1], ss[:, g:g + 1], float(D * EPS), None, ALU.add), reads=[bss], writes=[bss])
            op('pool', lambda e: e.tensor_tensor(ss[:, g:g + 1], ss[:, g:g + 1], cm05[:, 0:1], ALU.pow), reads=[bss, b_cm], writes=[bss])
            xn = cx['xn'][i]
            op('dve', lambda e: e.tensor_scalar(xn[:], xres[:, g, :], ss[:, g:g + 1], float(math.sqrt(D)), ALU.mult, ALU.mult),
               reads=[bx, bss], writes=[cx['b_xn'][i]])
            return i

        def norm_back(cx, g, i, xnT, b_xnT):
            xn = cx['xn'][i]
            ptr = cx['ptr'][i]
            for k in range(8):
                op('pe', lambda e, k=k: e.transpose(ptr[:, k, :], xn[:, k * 128:(k + 1) * 128], identb),
                   reads=[cx['b_xn'][i], b_cstb], writes=[cx['b_ptr'][i]])
            op('act', lambda e: e.copy(xnT[:, :, g * 128:(g + 1) * 128], ptr[:]), reads=[cx['b_ptr'][i]], writes=[b_xnT[g]])

        def norm_all(cx, xnT, b_xnT):
            pend = None
            for g in range(NG):
                i = norm_front(cx, g)
                if pend is not None:
                    norm_back(cx, pend[0], pend[1], xnT, b_xnT)
                pend = (g, i)
            norm_back(cx, pend[0], pend[1], xnT, b_xnT)

        def ffn(cx, ub, ub2, after_g, xnT, b_xnT):
            hT, w2 = cx['hT'], cx['w2']
            if cx.get('w2_loaded') != ub2:
                cx['w2_loaded'] = ub2
                for u in range(6):
                    nj = 4 if u < 5 else 2
                    fw.dma('pool', w2[:, u * 4:u * 4 + nj, :], WS[ub2 + u, :, 0:nj * 1024].rearrange("p (j c) -> p j c", j=nj),
                           reads=[b_ws[ub2 + u]], writes=[cx['b_w2'][u]])
            for u in range(11):
                rt, rb = ring_load(cx, ub + u)
                rv = rt[:].rearrange("p (k c) -> p k c", k=8)
                for jj in range(2):
                    j = 2 * u + jj
                    A, bA = next_acc(cx)
                    B, bB = next_acc(cx)
                    for kc in range(8):
                        op('pe', lambda e, kc=kc, A=A, jj=jj, rv=rv: e.matmul(A[:], rv[:, kc, jj * 128:(jj + 1) * 128], xnT[:, kc, :], start=(kc == 0), stop=(kc == 7)),
                           reads=[rb] + b_xnT, writes=[bA])
                    for kc in range(8):
                        op('pe', lambda e, kc=kc, B=B, jj=jj, rv=rv: e.matmul(B[:], rv[:, kc, 256 + jj * 128:256 + (jj + 1) * 128], xnT[:, kc, :], start=(kc == 0), stop=(kc == 7)),
                           reads=[rb] + b_xnT, writes=[bB])
                    si = j % 2
                    sg = cx['sg'][si]
                    op('act', lambda e, A=A, sg=sg: e.activation(sg[:], A[:], AF.Silu), reads=[bA], writes=[cx['b_sg'][si]])
                    op('dve', lambda e, B=B, sg=sg, j=j: e.tensor_tensor(hT[:, j, :], sg[:], B[:], ALU.mult),
                       reads=[cx['b_sg'][si], bB], writes=[cx['b_hT']])
            for g in range(NG):
                for c in range(2):
                    P, bP = next_acc(cx)
                    for j in range(NJ):
                        op('pe', lambda e, j=j, P=P, g=g, c=c: e.matmul(P[:], hT[:, j, g * 128:(g + 1) * 128], w2[:, j, c * 512:(c + 1) * 512], start=(j == 0), stop=(j == NJ - 1)),
                           reads=[cx['b_hT'], cx['b_w2'][j // 4]], writes=[bP])
                    xs = cx['xres'][:, g, c * 512:(c + 1) * 512]
                    op('dve', lambda e, P=P, xs=xs: e.scalar_tensor_tensor(xs, P[:], 0.5, xs, ALU.mult, ALU.add),
                       reads=[bP, cx['b_x'][g]], writes=[cx['b_x'][g]])
                after_g(g)

        with ExitStack() as st:
          if stage >= 1:
              cx = make_ffn_ctx(st, two_xnT=True)
              xnA, b_xnA = cx['xnT'], cx['b_xnT']
              xnB, b_xnB = cx['xnT2'], cx['b_xnT2']
              cv['stg'] = [SB(st, "stg%d" % i, [128, 4096], F32) for i in range(2)]; cv['b_stg'] = [Buf(), Buf()]
              cv['stb'] = SB(st, "stb", [128, 4096], BF16); cv['b_stb'] = Buf()
              pst = None
              for u_ in units_a:
                  nst = conv_load(u_)
                  if pst is not None:
                      conv_finish(pst)
                  pst = nst
              conv_finish(pst)
              cpend = [None]
              cleft = list(units_c)
              fst = [SB(st, "fst%d" % i, [128, 4, T], BF16) for i in range(2)]; b_fst = [Buf(), Buf()]
              zst = [SB(st, "zst%d" % i, [128, NG, 512], BF16) for i in range(2)]; b_zst = [Buf(), Buf()]
              gst = SB(st, "gst", [128, NG, 16], F32); b_gst = Buf()
              for ti in range(NTILE):
                  t0 = ti * T
                  def load_norm(tn):
                      for g in range(NG):
                          fw.dma('sp', cx['xres'][:, g, :], x_d[tn + g * 128:tn + (g + 1) * 128, :], writes=[cx['b_x'][g]])
                      norm_all(cx, xnA, b_xnA)
                  if ti == 0:
                      load_norm(t0)
                  pend1 = [None]

                  def after1(g, t0=t0):
                      fw.dma('pool', X1[t0 + g * 128:t0 + (g + 1) * 128, :], cx['xres'][:, g, :], reads=[cx['b_x'][g]])
                      i_ = norm_front(cx, g)
                      if pend1[0] is not None:
                          norm_back(cx, pend1[0][0], pend1[0][1], xnB, b_xnB)
                      pend1[0] = (g, i_)
                  ffn(cx, U_F1, U_F1W2, after1, xnA, b_xnA)
                  norm_back(cx, pend1[0][0], pend1[0][1], xnB, b_xnB)
                  if cpend[0] is not None:
                      conv_finish(cpend[0])
                      cpend[0] = None
                  while cleft and (cpend[0] is None):
                      cpend[0] = conv_load(cleft.pop(0))
                      if ti == NTILE - 1 or len(cleft) >= NTILE - 1 - ti:
                          conv_finish(cpend[0])
                          cpend[0] = None
                  if ti + 1 < NTILE:
                      load_norm(t0 + T)
                  xnT = xnB
                  fcount = 0
                  for (u, dst, c0, tanh) in [(U_QT, ZQT, 0, False), (U_K, ZKT, 0, False), (U_GM, GMT, 0, True), (U_GM + 1, GMT, 4, True),
                                             (U_GA, GAT, 0, True), (U_GA + 1, GAT, 4, True)]:
                      rt, rb = ring_load(cx, u)
                      rv = rt[:].rearrange("p (k c) -> p k c", k=8)
                      si = fcount % 2
                      fcount += 1
                      for cc in range(4):
                          P, bP = next_acc(cx)
                          for kc in range(8):
                              op('pe', lambda e, kc=kc, P=P, cc=cc, rv=rv: e.matmul(P[:], rv[:, kc, cc * 128:(cc + 1) * 128], xnT[:, kc, :], start=(kc == 0), stop=(kc == 7)),
                                 reads=[rb] + b_xnB, writes=[bP])
                          if tanh:
                              op('act', lambda e, P=P, cc=cc, si=si: e.activation(fst[si][:, cc, :], P[:], AF.Tanh, scale=0.5), reads=[bP], writes=[b_fst[si]])
                          else:
                              op('dve', lambda e, P=P, cc=cc, si=si: e.tensor_copy(fst[si][:, cc, :], P[:]), reads=[bP], writes=[b_fst[si]])
                      fw.dma('pool', dst[c0:c0 + 4, :, t0:t0 + T].rearrange("h p t -> p h t"), fst[si][:], reads=[b_fst[si]])
                  zc = 0
                  for (u, dst, ncol, kind) in [(U_K, ZK, 512, 'copy'), (U_V, ZV, 512, 'copy'), (U_MO, TMO, 512, 'tanh'),
                                               (U_AQ, ZAQ, 512, 'copy'), (U_KVG, ZAKV, 272, 'kvg')]:
                      rt, rb = ring_load(cx, u)
                      rv = rt[:].rearrange("p (k c) -> p k c", k=8)
                      si = zc % 2
                      zc += 1
                      for g in range(NG):
                          P, bP = next_acc(cx)
                          for kc in range(8):
                              op('pe', lambda e, kc=kc, P=P, g=g, rv=rv, ncol=ncol: e.matmul(P[:, 0:ncol], xnT[:, kc, g * 128:(g + 1) * 128], rv[:, kc, 0:ncol], start=(kc == 0), stop=(kc == 7)),
                                 reads=[rb, b_xnB[g]], writes=[bP])
                          if kind == 'tanh':
                              op('act', lambda e, P=P, g=g, si=si: e.activation(zst[si][:, g, :], P[:], AF.Tanh, scale=0.5), reads=[bP], writes=[b_zst[si]])
                          elif kind == 'copy':
                              if g % 2 == 0:
                                  op('dve', lambda e, P=P, g=g, si=si: e.tensor_copy(zst[si][:, g, :], P[:]), reads=[bP], writes=[b_zst[si]])
                              else:
                                  op('act', lambda e, P=P, g=g, si=si: e.copy(zst[si][:, g, :], P[:]), reads=[bP], writes=[b_zst[si]])
                          else:
                              op('dve', lambda e, P=P, g=g, si=si: e.tensor_copy(zst[si][:, g, 0:256], P[:, 0:256]), reads=[bP], writes=[b_zst[si]])
                              op('dve', lambda e, P=P, g=g: e.tensor_tensor(gst[:, g, :], P[:, 256:272], vecb[:, 0:16], ALU.add),
                                 reads=[bP, b_vec], writes=[b_gst])
                      if kind == 'kvg':
                          fw.dma('pool', ZAKV[t0:t0 + T, :].rearrange("(g p) c -> p g c", p=128), zst[si][:, :, 0:256], reads=[b_zst[si]])
                          fw.dma('pool', GT[:, t0 // 128:t0 // 128 + NG, :], gst[:], reads=[b_gst])
                      else:
                          fw.dma('pool', dst[t0:t0 + T, :].rearrange("(g p) c -> p g c", p=128), zst[si][:], reads=[b_zst[si]])
              fw.barrier()

        with ExitStack() as st:
          if stage >= 2:
            phase_b(nc, fw, st, SB, PS, seqs, dict(
                  cst=cst, b_cst=b_cst, cstb=cstb, b_cstb=b_cstb, vecb=vecb, b_vec=b_vec, cm05=cm05, b_cm=b_cm,
                ZQT=ZQT, ZKT=ZKT, ZK=ZK, ZV=ZV, TMO=TMO, ZAQ=ZAQ, ZAKV=ZAKV, GT=GT, HMT=HMT, HAT=HAT, rope_d=rope_d))
            fw.barrier()

        with ExitStack() as st:
          if stage >= 3:
              cx = make_ffn_ctx(st)
              hmt = SB(st, "hmt", [128, 4, T], BF16); b_hmt = Buf()
              hat = SB(st, "hat", [128, 4, T], BF16); b_hat = Buf()
              tgm = SB(st, "tgm", [128, 8, T], BF16); b_tgm = Buf()
              tga = SB(st, "tga", [128, 8, T], BF16); b_tga = Buf()
              mT = SB(st, "mT", [128, 8, T], BF16); b_mT = Buf()
              wres = [SB(st, "wres%d" % i, [128, 4096], BF16) for i in range(4)]; b_wres = [Buf() for _ in range(4)]
              m1 = [SB(st, "m1_%d" % i, [128, T], F32) for i in range(2)]; b_m1 = [Buf(), Buf()]
              m2 = [SB(st, "m2_%d" % i, [128, T], F32) for i in range(2)]; b_m2 = [Buf(), Buf()]
              for ti in range(NTILE):
                  t0 = ti * T
                  fw.dma('sp', hmt[:], HMT[:, :, t0:t0 + T].rearrange("h p t -> p h t"), writes=[b_hmt])
                  fw.dma('sp', hat[:], HAT[:, :, t0:t0 + T].rearrange("h p t -> p h t"), writes=[b_hat])
                  fw.dma('sp', tgm[:], GMT[:, :, t0:t0 + T].rearrange("h p t -> p h t"), writes=[b_tgm])
                  fw.dma('sp', tga[:], GAT[:, :, t0:t0 + T].rearrange("h p t -> p h t"), writes=[b_tga])
                  if ti == 0:
                      for (u_, t_, b_) in [(U_PM, wres[0], b_wres[0]), (U_PA, wres[1], b_wres[1]), (U_WO, wres[2], b_wres[2]), (U_WO + 1, wres[3], b_wres[3])]:
                          fw.dma('sp', t_[:], WS[u_, :, :], reads=[b_ws[u_]], writes=[b_])
                  rbm, rba = b_wres[0], b_wres[1]
                  vm = wres[0][:].rearrange("p (k c) -> p k c", k=4)
                  va = wres[1][:].rearrange("p (k c) -> p k c", k=4)
                  for oc in range(8):
                      A, bA = next_acc(cx)
                      B, bB = next_acc(cx)
                      for kc in range(4):
                          op('pe', lambda e, kc=kc, A=A, oc=oc: e.matmul(A[:], vm[:, kc, oc * 128:(oc + 1) * 128], hmt[:, kc, :], start=(kc == 0), stop=(kc == 3)),
                             reads=[rbm, b_hmt], writes=[bA])
                      for kc in range(4):
                          op('pe', lambda e, kc=kc, B=B, oc=oc: e.matmul(B[:], va[:, kc, oc * 128:(oc + 1) * 128], hat[:, kc, :], start=(kc == 0), stop=(kc == 3)),
                             reads=[rba, b_hat], writes=[bB])
                      si = oc % 2
                      op('dve', lambda e, A=A, oc=oc, si=si: e.scalar_tensor_tensor(m1[si][:], tgm[:, oc, :], 1.0, A[:], ALU.add, ALU.mult),
                         reads=[bA, b_tgm], writes=[b_m1[si]])
                      op('dve', lambda e, B=B, oc=oc, si=si: e.scalar_tensor_tensor(m2[si][:], tga[:, oc, :], 1.0, B[:], ALU.add, ALU.mult),
                         reads=[bB, b_tga], writes=[b_m2[si]])
                      op('pool', lambda e, oc=oc, si=si: e.tensor_tensor(mT[:, oc, :], m1[si][:], m2[si][:], ALU.add),
                         reads=[b_m1[si], b_m2[si]], writes=[b_mT])
                  for g in range(NG):
                      fw.dma('sp', cx['xres'][:, g, :], X1[t0 + g * 128:t0 + (g + 1) * 128, :], writes=[cx['b_x'][g]])
                  wo = [(wres[2 + c][:].rearrange("p (k c) -> p k c", k=8), b_wres[2 + c]) for c in range(2)]
                  pend2 = None
                  for g in range(NG):
                      for c in range(2):
                          rv, rb = wo[c]
                          P, bP = next_acc(cx)
                          for kc in range(8):
                              op('pe', lambda e, kc=kc, P=P, g=g, rv=rv: e.matmul(P[:], mT[:, kc, g * 128:(g + 1) * 128], rv[:, kc, :], start=(kc == 0), stop=(kc == 7)),
                                 reads=[rb, b_mT], writes=[bP])
                          xs = cx['xres'][:, g, c * 512:(c + 1) * 512]
                          op('dve', lambda e, P=P, xs=xs: e.tensor_tensor(xs, P[:], xs, ALU.add), reads=[bP, cx['b_x'][g]], writes=[cx['b_x'][g]])
                      i_ = norm_front(cx, g)
                      if pend2 is not None:
                          norm_back(cx, pend2[0], pend2[1], cx['xnT'], cx['b_xnT'])
                      pend2 = (g, i_)
                  norm_back(cx, pend2[0], pend2[1], cx['xnT'], cx['b_xnT'])

                  def after2(g, t0=t0):
                      fw.dma('pool', y_d[t0 + g * 128:t0 + (g + 1) * 128, :], cx['xres'][:, g, :], reads=[cx['b_x'][g]])
                  ffn(cx, U_F2, U_F2W2, after2, cx['xnT'], cx['b_xnT'])
              fw.barrier()
        print('total ops', fw.nops, {k: fw.cnt[k] for k in fw.names})
        fw.emit()
    return nc


def phase_b(nc, fw, st, SB, PS, seqs, G):
    op = fw.op
    cst, cstb, vecb, cm05 = G['cst'], G['cstb'], G['vecb'], G['cm05']
    b_cst, b_cstb, b_vec, b_cm = G['b_cst'], G['b_cstb'], G['b_vec'], G['b_cm']
    identb = cstb[:, K_ID:K_ID + 128]
    LNC = float(math.log(128.0 ** -0.5))
    SBK = 4

    gt = SB(st, "gt", [128, NBMAX, 16], F32); b_gt = Buf()
    l1 = SB(st, "l1", [128, NBMAX, 2, 4], F32); b_l1 = Buf()
    ui = SB(st, "ui", [128, NBMAX, 2, 4], F32); b_ui = Buf()
    UU = SB(st, "UU", [128, NBMAX, 2, 4], F32); b_UU = Buf()
    WW = SB(st, "WW", [128, NBMAX, 2, 4], F32); b_WW = Buf()
    EBI = SB(st, "EBI", [128, NBMAX, 2, 4], F32); b_EBI = Buf()
    EG = SB(st, "EG", [128, 2, NBMAX, 2, 4], F32); b_EG = Buf()
    lnc = SB(st, "lnc", [128, 1], F32); b_lnc = Buf()
    op('dve', lambda e: e.memset(lnc[:], LNC), writes=[b_lnc])
    hF = SB(st, "hF", [128, NBMAX, 512], BF16); b_hF = [Buf() for _ in range(NBMAX)]
    qT4 = [SB(st, "qT4_%d" % i, [128, 4, 512], BF16) for i in range(2)]; b_qT4 = [Buf(), Buf()]
    kT4 = [SB(st, "kT4_%d" % i, [128, 4, 512], BF16) for i in range(2)]; b_kT4 = [Buf(), Buf()]
    k4 = [SB(st, "k4_%d" % i, [128, SBK, 512], BF16) for i in range(2)]; b_k4 = [Buf(), Buf()]
    v4 = [SB(st, "v4_%d" % i, [128, SBK, 4, 132], BF16) for i in range(2)]; b_v4 = [[Buf() for _ in range(SBK)] for _ in range(2)]
    tmo4 = [SB(st, "tmo4_%d" % i, [128, SBK, 512], BF16) for i in range(2)]; b_tmo4 = [Buf(), Buf()]
    Cst = SB(st, "Cst", [128, 4, 132], F32); b_CA = Buf()
    Cb = SB(st, "Cb", [128, 4, 132], BF16); b_CbA = Buf()
    vw = [SB(st, "vw%d" % i, [128, 4, 132], BF16) for i in range(2)]; b_vw = [Buf(), Buf()]
    wq = [SB(st, "wq%d" % j, [128, 4, 128], BF16) for j in range(2)]; b_wq = [Buf(), Buf()]
    wtmp = SB(st, "wtmp", [128, 4, 128], F32); b_wtmp = Buf()
    dab = SB(st, "dab", [128, 8], F32); b_dab = Buf()
    hs = SB(st, "hs", [128, 4, 128], F32); b_hs = Buf()
    hsq = SB(st, "hsq", [128, 4, 128], F32); b_hsq = Buf()
    hss = SB(st, "hss", [128, 8], F32); b_hss = Buf()
    hn = SB(st, "hn", [128, 512], F32); b_hn = Buf()
    hmb = SB(st, "hmb", [128, 512], BF16); b_hmb = Buf()
    hst = [SB(st, "hst%d" % i, [128, 4, SBK * 128], BF16) for i in range(2)]; b_hst = [Buf(), Buf()]
    rope = SB(st, "rope", [128, NBMAX, 64], F32); b_rope = Buf()
    kTa = SB(st, "kTa", [64, 2, NBMAX * 128], BF16); b_kTa = [Buf() for _ in range(NBMAX)]
    va = SB(st, "va", [128, NBMAX, 2, 66], BF16); b_va = [Buf() for _ in range(NBMAX)]
    rsk = SB(st, "rsk", [128, NBMAX, 2], F32); b_rsk = [Buf() for _ in range(NBMAX)]
    aq4 = [SB(st, "aq4_%d" % i, [128, SBK, 512], BF16) for i in range(2)]; b_aq4 = [Buf(), Buf()]
    ak4 = [SB(st, "ak4_%d" % i, [128, SBK, 128], BF16) for i in range(2)]; b_ak4 = [Buf(), Buf()]
    kf = SB(st, "kf", [128, SBK, 2, 64], F32); b_kf = Buf()
    ksq = SB(st, "ksq", [128, SBK, 2, 64], F32); b_ksq = Buf()
    kss = SB(st, "kss", [128, SBK, 2], F32); b_kss = Buf()
    kr = SB(st, "kr", [128, SBK, 2, 64], BF16); b_kr = Buf()
    kt1 = SB(st, "kt1", [128, SBK, 2, 32], F32); b_kt1 = Buf()
    kt2 = SB(st, "kt2", [128, SBK, 2, 32], F32); b_kt2 = Buf()
    qf = SB(st, "qf", [128, SBK, 8, 64], F32); b_qf = Buf()
    qsq = SB(st, "qsq", [128, SBK, 8, 64], F32); b_qsq = Buf()
    qss = SB(st, "qss", [128, SBK, 8], F32); b_qss = Buf()
    qr = [SB(st, "qr%d" % i, [128, SBK, 8, 64], BF16) for i in range(2)]; b_qr = [Buf(), Buf()]
    qt1 = qsq[:, 0:2, :, :].rearrange("p a h c -> p (a h c)").rearrange("p (a h c) -> p a h c", a=SBK, h=8); b_qt1 = b_qsq
    qt2 = qsq[:, 2:4, :, :].rearrange("p a h c -> p (a h c)").rearrange("p (a h c) -> p a h c", a=SBK, h=8); b_qt2 = b_qsq
    qTa = SB(st, "qTa", [64, 8, 128], BF16); b_qTa = Buf()
    pp = [SB(st, "pp%d" % i, [128, 512], BF16) for i in range(3)]; b_pp = [Buf() for _ in range(3)]
    esk = SB(st, "esk", [128, 8], F32); b_esk = Buf()
    rden = SB(st, "rden", [128, 8], F32); b_rden = Buf()
    hab = SB(st, "hab", [128, 8, 64], BF16); b_hab = Buf()
    hat_st = [SB(st, "hatst%d" % i, [128, 4, SBK * 128], BF16) for i in range(2)]; b_hatst = [Buf(), Buf()]
    pg = SB(st, "pgs", [128, 1024], F32); b_pg = Buf()
    psc = PS(st, "psc", [128, 4, 128], F32); _b = Buf(True); b_psc = [_b] * 4
    pnd = PS(st, "pnd", [128, 4, 256], F32); _a, _b2 = Buf(True), Buf(True); b_pnd = [_a, _a, _b2, _b2]
    pdc = PS(st, "pdc", [128, 4, 256], F32); _c, _d = Buf(True), Buf(True); b_pdc = [_c, _c, _d, _d]
    ptb = PS(st, "ptb", [128, 8, 128], BF16); b_ptb = Buf(True)
    pdcf = pdc[:].rearrange("p h c -> p (h c)")
    pgp = pdcf; b_pg0 = b_pdc[0]; b_pg1 = b_pdc[2]; b_pgp = [b_pg0, b_pg1]
    asc = PS(st, "asc", [128, 512], F32); b_asc = Buf(True)
    apo = PS(st, "apo", [128, 4, 128], F32); b_apo = Buf(True)
    for i in range(2):
        op('dve', lambda e, i=i: e.memset(v4[i][:], 1.0), writes=b_v4[i])
    op('dve', lambda e: e.memset(va[:], 1.0), writes=b_va)
    fw.dma('sp', rope[:], G['rope_d'][:, :, :], writes=[b_rope])
    op('act', lambda e: e.activation(esk[:], vecb[:, 144:152], AF.Exp), reads=[b_vec], writes=[b_esk])
    qg = vecb[:, 16:80]
    kg = vecb[:, 80:144]

    tok0 = 0
    for S in seqs:
        NB = S // 128
        blk0 = tok0 // 128
        NSB = NB // SBK
        fw.dma('sp', gt[:, 0:NB, :], G['GT'][:, blk0:blk0 + NB, :], writes=[b_gt])
        gtv = gt[:, 0:NB, :].rearrange("p n (t h) -> p n t h", t=4)
        for d in range(2):
            op('act', lambda e, d=d: e.activation(l1[:, 0:NB, d, :], gtv[:, :, 1 + 2 * d, :], AF.Exp, scale=-1.0), reads=[b_gt], writes=[b_l1])
        op('act', lambda e: e.activation(l1[:, 0:NB, :, :], l1[:, 0:NB, :, :], AF.Ln, bias=1.0), reads=[b_l1], writes=[b_l1])
        n4 = NB * 4
        pgv = pg[:, 0:2 * n4].rearrange("p (d n h) -> p d n h", d=2, h=4)
        pgt = pg[:, 256:256 + 2 * n4].rearrange("p (d n h) -> p d n h", d=2, h=4)
        pge = pg[:, 512:512 + 4 * n4].rearrange("p (c d n h) -> p c d n h", c=2, d=2, h=4)
        Pgv = pgp[:, 0:2 * n4].rearrange("p (d n h) -> p d n h", d=2, h=4)
        Pgt = pgp[:, 256:256 + 2 * n4].rearrange("p (d n h) -> p d n h", d=2, h=4)
        Pge = pgp[:, 512:512 + 4 * n4].rearrange("p (c d n h) -> p c d n h", c=2, d=2, h=4)
        for d in range(2):
            msk = cst[:, (K_MF if d == 0 else K_MB):(K_MF if d == 0 else K_MB) + 128]
            op('pe', lambda e, d=d, msk=msk: e.matmul(Pgv[:, d, :, :], msk, l1[:, 0:NB, d, :], start=True, stop=True),
               reads=[b_cst, b_l1], writes=[b_pg0])
            op('pe', lambda e, d=d: e.matmul(Pgt[:, d, :, :], cst[:, K_BLK:K_BLK + 128], l1[:, 0:NB, d, :], start=True, stop=True),
               reads=[b_cst, b_l1], writes=[b_pg0])
            for c in range(2):
                op('pe', lambda e, d=d, c=c: e.matmul(Pge[:, c, d, :, :], cst[:, K_S0 + c * 128:K_S0 + (c + 1) * 128], l1[:, 0:NB, d, :], start=True, stop=True),
                   reads=[b_cst, b_l1], writes=[b_pg1])
        op('act', lambda e: e.copy(pg[:], pgp[:]), reads=b_pgp, writes=[b_pg])
        for d in range(2):
            op('dve', lambda e, d=d: e.tensor_tensor(ui[:, 0:NB, d, :], gtv[:, :, 2 * d, :], pgv[:, d, :, :], ALU.add), reads=[b_gt, b_pg], writes=[b_ui])
            op('act', lambda e, d=d: e.activation(UU[:, 0:NB, d, :], ui[:, 0:NB, d, :], AF.Exp, bias=lnc[:, 0:1]), reads=[b_ui, b_lnc], writes=[b_UU])
            op('act', lambda e, d=d: e.activation(EBI[:, 0:NB, d, :], pgv[:, d, :, :], AF.Exp), reads=[b_pg], writes=[b_EBI])
        for d in range(2):
            op('dve', lambda e, d=d: e.tensor_tensor(ui[:, 0:NB, d, :], ui[:, 0:NB, d, :], pgt[:, d, :, :], ALU.subtract), reads=[b_ui, b_pg, b_UU], writes=[b_ui])
            op('act', lambda e, d=d: e.activation(WW[:, 0:NB, d, :], ui[:, 0:NB, d, :], AF.Exp, bias=lnc[:, 0:1]), reads=[b_ui, b_lnc], writes=[b_WW])
            for c in range(2):
                op('act', lambda e, d=d, c=c: e.activation(EG[:, c, 0:NB, d, :], pge[:, c, d, :, :], AF.Exp, scale=-1.0), reads=[b_pg], writes=[b_EG])

        def gen_m():
            for d in range(2):
                op('dve', lambda e: e.memset(Cst[:], 0.0), writes=[b_CA])
                op('dve', lambda e: e.memset(Cb[:], 0.0), writes=[b_CbA])
                sb_order = list(range(NSB)) if d == 0 else list(range(NSB - 1, -1, -1))
                mask = cst[:, (K_MF if d == 0 else K_MB):(K_MF if d == 0 else K_MB) + 128]
                corder = [0, 1] if d == 0 else [1, 0]

                def loads(sidx):
                    sbi = sb_order[sidx]
                    bi = sidx % 2
                    ts = tok0 + sbi * SBK * 128
                    fw.dma('sp', qT4[bi][:], G['ZQT'][:, :, ts:ts + 512].rearrange("h p t -> p h t"), writes=[b_qT4[bi]])
                    fw.dma('sp', kT4[bi][:], G['ZKT'][:, :, ts:ts + 512].rearrange("h p t -> p h t"), writes=[b_kT4[bi]])
                    fw.dma('sp', k4[bi][:], G['ZK'][ts:ts + 512, :].rearrange("(b p) c -> p b c", p=128), writes=[b_k4[bi]])
                    for b_ in range(SBK):
                        fw.dma('sp', v4[bi][:, b_, :, 0:128], G['ZV'][ts + b_ * 128:ts + (b_ + 1) * 128, :].rearrange("p (h c) -> p h c", h=4), writes=[b_v4[bi][b_]])
                    if d == 1:
                        fw.dma('sp', tmo4[bi][:], G['TMO'][ts:ts + 512, :].rearrange("(b p) c -> p b c", p=128), writes=[b_tmo4[bi]])

                blist = []
                for sidx, sbi in enumerate(sb_order):
                    for bb in (range(SBK) if d == 0 else range(SBK - 1, -1, -1)):
                        blist.append((sidx, sbi, bb))

                def stage_s(i):
                    sidx, sbi, bb = blist[i]
                    bi = sidx % 2
                    nb = sbi * SBK + bb
                    cols = slice(bb * 128, (bb + 1) * 128)
                    vi = i % 2
                    op('pool', lambda e: e.tensor_tensor(vw[vi][:, :, 0:129], v4[bi][:, bb, :, 0:129],
                                                         WW[:, nb, d, :].unsqueeze(2).to_broadcast([128, 4, 129]), ALU.mult),
                       reads=[b_v4[bi][bb], b_WW], writes=[b_vw[vi]])
                    for h in range(4):
                        op('pe', lambda e, h=h: e.matmul(psc[:, h, :], kT4[bi][:, h, cols], qT4[bi][:, h, cols], start=True, stop=True),
                           reads=[b_kT4[bi], b_qT4[bi]], writes=[b_psc[h]])
                    op('dve', lambda e: e.tensor_tensor(wtmp[:], psc[:], UU[:, nb, d, :].unsqueeze(2).to_broadcast([128, 4, 128]), ALU.mult),
                       reads=[b_psc[0], b_UU], writes=[b_wtmp])
                    op('pool', lambda e: e.tensor_tensor(wq[vi][:], wtmp[:], mask.unsqueeze(1).to_broadcast([128, 4, 128]), ALU.mult),
                       reads=[b_wtmp, b_cst], writes=[b_wq[vi]])
                    yield

                def stage_r(i, c):
                    sidx, sbi, bb = blist[i]
                    bi = sidx % 2
                    nb = sbi * SBK + bb
                    vi = i % 2
                    rows = slice(c * 64, (c + 1) * 64)
                    for h in (0, 2, 1, 3):
                        op('pe', lambda e, h=h: e.matmul(pnd[rows, h, 0:129], wq[vi][rows, h, rows], v4[bi][rows, bb, h, 0:129], start=True, stop=False),
                           reads=[b_wq[vi], b_v4[bi][bb]], writes=[b_pnd[h]])
                        op('pe', lambda e, h=h: e.matmul(pnd[rows, h, 0:129], qT4[bi][:, h, bb * 128 + c * 64:bb * 128 + (c + 1) * 64], Cb[:, h, 0:129], start=False, stop=True),
                           reads=[b_qT4[bi], b_CbA], writes=[b_pnd[h]])
                        op('pe', lambda e, h=h: e.matmul(pdc[:, h, 0:129], k4[bi][rows, bb, h * 128:(h + 1) * 128], vw[vi][rows, h, 0:129], start=True, stop=True),
                           reads=[b_k4[bi], b_vw[vi]], writes=[b_pdc[h]])
                    yield
                    op('dve', lambda e: e.tensor_tensor(Cst[:, :, 0:129], Cst[:, :, 0:129], EG[:, c, nb, d, :].unsqueeze(2).to_broadcast([128, 4, 129]), ALU.mult),
                       reads=[b_CA, b_EG], writes=[b_CA])
                    op('dve', lambda e: e.tensor_tensor(Cst[:, :, 0:129], Cst[:, :, 0:129], pdc[:, :, 0:129], ALU.add),
                       reads=[b_CA, b_pdc[0], b_pdc[2]], writes=[b_CA])
                    op('act', lambda e: e.copy(Cb[:, :, 0:129], Cst[:, :, 0:129]), reads=[b_CA], writes=[b_CbA])
                    yield

                def stage_o(i):
                    sidx, sbi, bb = blist[i]
                    bi = sidx % 2
                    nb = sbi * SBK + bb
                    ts = tok0 + sbi * SBK * 128
                    op('act', lambda e: e.activation(dab[:, 0:4], pnd[:, :, 128], AF.Abs), reads=b_pnd, writes=[b_dab])
                    op('dve', lambda e: e.tensor_tensor(dab[:, 0:4], dab[:, 0:4], EBI[:, nb, d, :], ALU.max), reads=[b_dab, b_EBI], writes=[b_dab])
                    op('dve', lambda e: e.reciprocal(dab[:, 0:4], dab[:, 0:4]), reads=[b_dab], writes=[b_dab])
                    yield
                    dabb = dab[:, 0:4].unsqueeze(2).to_broadcast([128, 4, 128])
                    if d == 0:
                        op('dve', lambda e: e.tensor_tensor(hF[:, nb, :].rearrange("p (h c) -> p h c", h=4), pnd[:, :, 0:128], dabb, ALU.mult),
                           reads=b_pnd + [b_dab], writes=[b_hF[nb]])
                        yield
                    else:
                        op('dve', lambda e: e.tensor_tensor(hs[:], pnd[:, :, 0:128], dabb, ALU.mult), reads=b_pnd + [b_dab], writes=[b_hs])
                        op('pool', lambda e: e.tensor_tensor(hs[:], hs[:], hF[:, nb, :].rearrange("p (h c) -> p h c", h=4), ALU.add),
                           reads=[b_hs, b_hF[nb]], writes=[b_hs])
                        yield
                        op('pool', lambda e: e.tensor_tensor(hsq[:], hs[:], hs[:], ALU.mult), reads=[b_hs], writes=[b_hsq])
                        op('dve', lambda e: e.tensor_reduce(hss[:, 0:4], hsq[:], AX.X, ALU.add), reads=[b_hsq], writes=[b_hss])
                        op('dve', lambda e: e.tensor_scalar(hss[:, 0:4], hss[:, 0:4], float(128 * EPS), None, ALU.add), reads=[b_hss], writes=[b_hss])
                        op('pool', lambda e: e.tensor_tensor(hss[:, 0:4], hss[:, 0:4], cm05[:, 0:4], ALU.pow), reads=[b_hss, b_cm], writes=[b_hss])
                        yield
                        op('dve', lambda e: e.tensor_tensor(hn[:].rearrange("p (h c) -> p h c", h=4), hs[:], hss[:, 0:4].unsqueeze(2).to_broadcast([128, 4, 128]), ALU.mult),
                           reads=[b_hs, b_hss], writes=[b_hn])
                        op('dve', lambda e: e.scalar_tensor_tensor(hn[:], tmo4[bi][:, bb, :], 1.0, hn[:], ALU.add, ALU.mult),
                           reads=[b_tmo4[bi], b_hn], writes=[b_hn])
                        op('act', lambda e: e.activation(hmb[:], hn[:], AF.Copy, scale=float(math.sqrt(128.0))), reads=[b_hn], writes=[b_hmb])
                        yield
                        for kc in range(4):
                            op('pe', lambda e, kc=kc: e.transpose(ptb[:, kc, :], hmb[:, kc * 128:(kc + 1) * 128], identb), reads=[b_hmb, b_cstb], writes=[b_ptb])
                        op('act', lambda e: e.copy(hst[bi][:, :, bb * 128:(bb + 1) * 128], ptb[:, 0:4, :]), reads=[b_ptb], writes=[b_hst[bi]])
                        yield
                        if i + 1 == len(blist) or blist[i + 1][0] != sidx:
                            fw.dma('pool', G['HMT'][:, :, ts:ts + 512].rearrange("h p t -> p h t"), hst[bi][:], reads=[b_hst[bi]])

                loads(0)
                yield from stage_s(0)
                for i in range(len(blist)):
                    sidx = blist[i][0]
                    if (i == 0 or blist[i - 1][0] != sidx) and sidx + 1 < NSB:
                        loads(sidx + 1)
                    yield from stage_r(i, corder[0])
                    if i + 1 < len(blist):
                        yield from stage_s(i + 1)
                    yield from stage_r(i, corder[1])
                    yield from stage_o(i)

        def gen_a():
            for sbi in range(NSB):
                bi = sbi % 2
                ts = tok0 + sbi * SBK * 128
                fw.dma('sp', ak4[bi][:], G['ZAKV'][ts:ts + 512, 0:128].rearrange("(b p) c -> p b c", p=128), writes=[b_ak4[bi]])
                for b_ in range(SBK):
                    fw.dma('sp', va[:, sbi * SBK + b_, :, 0:64],
                           G['ZAKV'][ts + b_ * 128:ts + (b_ + 1) * 128, 128:256].rearrange("p (h c) -> p h c", h=2), writes=[b_va[sbi * SBK + b_]])
                kin = ak4[bi][:].rearrange("p b (h c) -> p b h c", h=2)
                nbs = slice(sbi * SBK, (sbi + 1) * SBK)
                op('pool', lambda e: e.tensor_tensor(ksq[:], kin, kin, ALU.mult), reads=[b_ak4[bi]], writes=[b_ksq])
                op('dve', lambda e: e.tensor_reduce(kss[:], ksq[:], AX.X, ALU.add), reads=[b_ksq], writes=[b_kss])
                op('dve', lambda e: e.tensor_scalar(kss[:], kss[:], float(64 * EPS), None, ALU.add), reads=[b_kss], writes=[b_kss])
                op('pool', lambda e: e.tensor_tensor(rsk[:, nbs, :], kss[:], cm05[:, 0:SBK * 2].rearrange("p (b h) -> p b h", h=2), ALU.pow),
                   reads=[b_kss, b_cm], writes=[b_rsk[n_] for n_ in range(sbi * SBK, (sbi + 1) * SBK)])
                yield
                op('pool', lambda e: e.tensor_tensor(kf[:], kin, kg.unsqueeze(1).unsqueeze(1).to_broadcast([128, SBK, 2, 64]), ALU.mult), reads=[b_ak4[bi], b_vec], writes=[b_kf])
                cosb = rope[:, nbs, 0:32].unsqueeze(2).to_broadcast([128, SBK, 2, 32])
                sinb = rope[:, nbs, 32:64].unsqueeze(2).to_broadcast([128, SBK, 2, 32])
                op('pool', lambda e: e.tensor_tensor(kt1[:], kf[:, :, :, 0:32], cosb, ALU.mult), reads=[b_kf, b_rope], writes=[b_kt1])
                op('pool', lambda e: e.tensor_tensor(kt2[:], kf[:, :, :, 32:64], sinb, ALU.mult), reads=[b_kf, b_rope], writes=[b_kt2])
                op('pool', lambda e: e.tensor_tensor(kr[:, :, :, 0:32], kt1[:], kt2[:], ALU.subtract), reads=[b_kt1, b_kt2], writes=[b_kr])
                yield
                op('pool', lambda e: e.tensor_tensor(kt1[:], kf[:, :, :, 32:64], cosb, ALU.mult), reads=[b_kf, b_rope, b_kr], writes=[b_kt1])
                op('pool', lambda e: e.tensor_tensor(kt2[:], kf[:, :, :, 0:32], sinb, ALU.mult), reads=[b_kf, b_rope, b_kr], writes=[b_kt2])
                op('pool', lambda e: e.tensor_tensor(kr[:, :, :, 32:64], kt1[:], kt2[:], ALU.add), reads=[b_kt1, b_kt2], writes=[b_kr])
                yield
                for bb in range(SBK):
                    nb = sbi * SBK + bb
                    for h in range(2):
                        op('pe', lambda e, h=h: e.transpose(ptb[0:64, h, :], kr[:, bb, h, :], identb), reads=[b_kr, b_cstb], writes=[b_ptb])
                    op('act', lambda e: e.copy(kTa[:, :, nb * 128:(nb + 1) * 128], ptb[0:64, 0:2, :]), reads=[b_ptb], writes=[b_kTa[nb]])
                    yield
            def q_front_a(sbi):
                bi = sbi % 2
                ts = tok0 + sbi * SBK * 128
                nbs = slice(sbi * SBK, (sbi + 1) * SBK)
                qro, b_qro = qr[bi], b_qr[bi]
                fw.dma('sp', aq4[bi][:], G['ZAQ'][ts:ts + 512, :].rearrange("(b p) c -> p b c", p=128), writes=[b_aq4[bi]])
                qin = aq4[bi][:].rearrange("p b (h c) -> p b h c", h=8)
                op('pool', lambda e: e.tensor_tensor(qsq[:], qin, qin, ALU.mult), reads=[b_aq4[bi]], writes=[b_qsq])
                op('dve', lambda e: e.tensor_reduce(qss[:], qsq[:], AX.X, ALU.add), reads=[b_qsq], writes=[b_qss])
                op('dve', lambda e: e.tensor_scalar(qss[:], qss[:], float(64 * EPS), None, ALU.add), reads=[b_qss], writes=[b_qss])
                op('pool', lambda e: e.tensor_tensor(qss[:], qss[:], cm05[:, 0:SBK * 8].rearrange("p (b h) -> p b h", h=8), ALU.pow), reads=[b_qss, b_cm], writes=[b_qss])
                yield
                op('dve', lambda e: e.tensor_tensor(qf[:], qin, qss[:].unsqueeze(3).to_broadcast([128, SBK, 8, 64]), ALU.mult), reads=[b_aq4[bi], b_qss], writes=[b_qf])
                op('dve', lambda e: e.scalar_tensor_tensor(qf[:].rearrange("p a h c -> p (a h) c"), qf[:].rearrange("p a h c -> p (a h) c"), 8.0, qg.unsqueeze(1).to_broadcast([128, SBK * 8, 64]), ALU.mult, ALU.mult), reads=[b_qf, b_vec], writes=[b_qf])
                yield
                cosb = rope[:, nbs, 0:32].unsqueeze(2).to_broadcast([128, SBK, 8, 32])
                sinb = rope[:, nbs, 32:64].unsqueeze(2).to_broadcast([128, SBK, 8, 32])
                op('dve', lambda e: e.tensor_tensor(qt1, qf[:, :, :, 0:32], cosb, ALU.mult), reads=[b_qf, b_rope, b_qss], writes=[b_qt1])
                op('pool', lambda e: e.tensor_tensor(qt2, qf[:, :, :, 32:64], sinb, ALU.mult), reads=[b_qf, b_rope, b_qss], writes=[b_qt2])
                op('dve', lambda e: e.tensor_tensor(qro[:, :, :, 0:32], qt1, qt2, ALU.subtract), reads=[b_qt1, b_qt2], writes=[b_qro])
                yield
                op('dve', lambda e: e.tensor_tensor(qt1, qf[:, :, :, 32:64], cosb, ALU.mult), reads=[b_qf, b_rope, b_qro], writes=[b_qt1])
                op('pool', lambda e: e.tensor_tensor(qt2, qf[:, :, :, 0:32], sinb, ALU.mult), reads=[b_qf, b_rope, b_qro], writes=[b_qt2])
                op('dve', lambda e: e.tensor_tensor(qro[:, :, :, 32:64], qt1, qt2, ALU.add), reads=[b_qt1, b_qt2], writes=[b_qro])
                yield

            def q_front_b(nb):
                sbi, bb = nb // SBK, nb % SBK
                bi = sbi % 2
                for h in range(8):
                    op('pe', lambda e, h=h: e.transpose(ptb[0:64, h, :], qr[bi][:, bb, h, :], identb), reads=[b_qr[bi], b_cstb], writes=[b_ptb])
                op('act', lambda e: e.copy(qTa[:], ptb[0:64, :, :]), reads=[b_ptb], writes=[b_qTa])
                yield

            def q_back(nb):
                sbi, bb = nb // SBK, nb % SBK
                bi = sbi % 2
                ts = tok0 + sbi * SBK * 128
                kbs = [kb for kb in (nb - 1, nb, nb + 1) if 0 <= kb < NB]
                for kvh in range(2):
                    for ki, kb in enumerate(kbs):
                        op('pe', lambda e, kb=kb: e.matmul(asc[:], kTa[:, kvh, kb * 128:(kb + 1) * 128],
                                                           qTa[:, kvh * 4:(kvh + 1) * 4, :], start=True, stop=True),
                           reads=[b_kTa[kb], b_qTa], writes=[b_asc])
                        op('act', lambda e, ki=ki, kb=kb: e.activation(pp[ki][:], asc[:], AF.Exp, scale=rsk[:, kb, kvh:kvh + 1]),
                           reads=[b_asc, b_rsk[kb]], writes=[b_pp[ki]])
                        yield
                        if kb != nb:
                            mk = cstb[:, (K_GE if kb < nb else K_LE):(K_GE if kb < nb else K_LE) + 128]
                            op('pool', lambda e, ki=ki, mk=mk: e.tensor_tensor(pp[ki][:].rearrange("p (g q) -> p g q", g=4), pp[ki][:].rearrange("p (g q) -> p g q", g=4),
                                                                              mk.unsqueeze(1).to_broadcast([128, 4, 128]), ALU.mult),
                               reads=[b_pp[ki], b_cstb], writes=[b_pp[ki]])
                    for g in range(4):
                        for ki, kb in enumerate(kbs):
                            op('pe', lambda e, g=g, ki=ki, kb=kb: e.matmul(apo[:, g, 0:65], pp[ki][:, g * 128:(g + 1) * 128], va[:, kb, kvh, 0:65],
                                                                          start=(ki == 0), stop=(ki == len(kbs) - 1)),
                               reads=[b_pp[ki], b_va[kb]], writes=[b_apo])
                    yield
                    hsl = slice(kvh * 4, (kvh + 1) * 4)
                    op('dve', lambda e: e.tensor_tensor(rden[:, hsl], apo[:, :, 64], esk[:, hsl], ALU.add), reads=[b_apo, b_esk], writes=[b_rden])
                    op('dve', lambda e: e.reciprocal(rden[:, hsl], rden[:, hsl]), reads=[b_rden], writes=[b_rden])
                    op('dve', lambda e: e.tensor_tensor(hab[:, hsl, :], apo[:, :, 0:64], rden[:, hsl].unsqueeze(2).to_broadcast([128, 4, 64]), ALU.mult),
                       reads=[b_apo, b_rden], writes=[b_hab])
                    yield
                habf = hab[:].rearrange("p h c -> p (h c)")
                for kc in range(4):
                    op('pe', lambda e, kc=kc: e.transpose(ptb[:, kc, :], habf[:, kc * 128:(kc + 1) * 128], identb), reads=[b_hab, b_cstb], writes=[b_ptb])
                op('act', lambda e: e.copy(hat_st[bi][:, :, bb * 128:(bb + 1) * 128], ptb[:, 0:4, :]), reads=[b_ptb], writes=[b_hatst[bi]])
                yield
                if bb == SBK - 1:
                    fw.dma('pool', G['HAT'][:, :, ts:ts + 512].rearrange("h p t -> p h t"), hat_st[bi][:], reads=[b_hatst[bi]])

            yield from q_front_a(0)
            yield from q_front_b(0)
            for nb in range(NB):
                if nb % SBK == 0 and nb // SBK + 1 < NSB:
                    yield from q_front_a(nb // SBK + 1)
                yield from q_back(nb)
                if nb + 1 < NB:
                    yield from q_front_b(nb + 1)

        gens = [gen_m(), gen_a()]
        while gens:
            for g_ in list(gens):
                try:
                    next(g_)
                except StopIteration:
                    gens.remove(g_)
        tok0 += S


def make_consts():
    c = np.zeros((128, NCST), np.float32)
    i = np.arange(128)
    s, t = i[:, None], i[None, :]
    same = (s // 64) == (t // 64)
    c[:, K_ID:K_ID + 128] = np.eye(128)
    c[:, K_LE:K_LE + 128] = (s <= t)
    c[:, K_GE:K_GE + 128] = (s >= t)
    c[:, K_BLK:K_BLK + 128] = same
    c[:, K_MF:K_MF + 128] = same & (s <= t)
    c[:, K_MB:K_MB + 128] = same & (s >= t)
    c[:, K_ONE:K_ONE + 128] = 1.0
    c[0:64, K_S0:K_S0 + 128] = 1.0
    c[64:128, K_S1:K_S1 + 128] = 1.0
    return c


def make_rope():
    half = 32
    inv = np.power(np.float32(10000.0), -np.arange(half, dtype=np.float32) / np.float32(half)).astype(np.float32)
    pos = np.arange(NBMAX * 128, dtype=np.float32)
    ang = (pos[:, None] * inv[None, :]).astype(np.float32)
    r = np.concatenate([np.cos(ang), np.sin(ang)], axis=-1).astype(np.float32)
    return np.ascontiguousarray(r.reshape(NBMAX, 128, 64).transpose(1, 0, 2))


def common_inputs(inp):
    f = lambda k: np.ascontiguousarray(np.asarray(inp[k], np.float32)[0])
    gcol = np.concatenate([f('ffn1_norm').reshape(8, 128).T, f('mix_norm').reshape(8, 128).T,
                           f('ffn2_norm').reshape(8, 128).T, f('m_norm').reshape(4, 128).T], axis=1)
    vec = np.zeros((128, 160), np.float32)
    vec[:, 0:16] = f('b_gates'); vec[:, 16:80] = f('q_norm'); vec[:, 80:144] = f('k_norm'); vec[:, 144:152] = f('sink')
    return {
        "f1w1": f('ffn1_w1'), "f1w3": f('ffn1_w3'), "f1w2": f('ffn1_w2'),
        "f2w1": f('ffn2_w1'), "f2w3": f('ffn2_w3'), "f2w2": f('ffn2_w2'),
        "win": f('w_in'), "wpm": f('w_pm'), "wpa": f('w_pa'), "wout": f('w_out'),
        "gcol": np.ascontiguousarray(gcol), "vec": vec, "cst": make_consts(), "rope": make_rope(),
    }


def kernel(**inputs):
    xp = np.asarray(inputs['x_prompt'], np.float32)
    xs = np.asarray(inputs['x_sample'], np.float32)
    seqs = [SEQ] * PROMPT_PER_CORE + [DEC_SEQ]
    nc = build(seqs)
    com = common_inputs(inputs)
    in_maps = []
    for c in range(N_CORES):
        xc = np.concatenate([xp[c * PROMPT_PER_CORE:(c + 1) * PROMPT_PER_CORE].reshape(-1, D), xs[c]], axis=0)
        m = dict(com)
        m["x"] = np.ascontiguousarray(xc)
        in_maps.append(m)
    res = run_bass_kernel_spmd(nc, in_maps, core_ids=list(range(N_CORES)))
    yp = np.empty_like(xp)
    ys = np.empty_like(xs)
    npt = PROMPT_PER_CORE * SEQ
    for c in range(N_CORES):
        y = res.results[c]["y"]
        yp[c * PROMPT_PER_CORE:(c + 1) * PROMPT_PER_CORE] = y[:npt].reshape(PROMPT_PER_CORE, SEQ, D)
        ys[c] = y[npt:]
    return (yp, ys)
```

```python
import math
import types
from contextlib import ExitStack
import numpy as np
import concourse.bass as bass
import concourse.mybir as mybir
from concourse.bass_utils import run_bass_kernel_spmd

F32 = mybir.dt.float32
BF16 = mybir.dt.bfloat16
AF = mybir.ActivationFunctionType
ALU = mybir.AluOpType
AX = mybir.AxisListType

D = 1024
DFF = 2816
NJ = 22
EPS = 1e-6
NG = 4
T = NG * 128
N_CORES = 8
PROMPT_PER_CORE = 4
SEQ = 2048
DEC_SEQ = 4096
NBMAX = 32
C_MQ, C_MK, C_MV, C_MO, C_MG, C_AQ, C_AK, C_AV, C_GM, C_GA = 0, 512, 1024, 1536, 2048, 2064, 2576, 2704, 2832, 3856
U_F1 = 0
U_F1W2 = 11
U_QT = 17
U_K = 18
U_GM = 19
U_GA = 21
U_V = 23
U_MO = 24
U_AQ = 25
U_KVG = 26
U_PM = 27
U_PA = 28
U_WO = 29
U_F2 = 31
U_F2W2 = 42
NUNITS = 48
K_ID, K_LE, K_GE, K_BLK, K_MF, K_MB, K_ONE, K_S0, K_S1 = 0, 128, 256, 384, 512, 640, 768, 896, 1024
NCST = 1152


def _freeze(fn):
    if fn.__closure__ is None:
        return fn
    cells = []
    for c in fn.__closure__:
        try:
            cells.append(types.CellType(c.cell_contents))
        except ValueError:
            cells.append(c)
    return types.FunctionType(fn.__code__, fn.__globals__, fn.__name__, fn.__defaults__, tuple(cells))


class Buf:
    __slots__ = ("w", "r", "excl")

    def __init__(self, excl=False):
        self.w = None
        self.r = {}
        self.excl = excl


class FW:
    def __init__(self, nc, stack):
        self.nc = nc
        self.stack = stack
        self.names = ['pe', 'act', 'dve', 'pool', 'sp']
        self.sem = {}
        self.cnt = {}
        self.known = {k: {} for k in self.names}
        self.nsem = 0
        self._new_sems()
        self.NDS = 8
        self.dsem = {}
        self.dcnt = {}
        self.dnext = {}
        for q in ('sp', 'pool'):
            self.dsem[q] = [stack.enter_context(nc.semaphore('d_%s%d' % (q, i))) for i in range(self.NDS)]
            self.dcnt[q] = [0] * self.NDS
            self.dnext[q] = 0
        self.ops = {k: [] for k in self.names}
        self.nops = 0
        self.maxops = 10 ** 9

    def _new_sems(self):
        for k in self.names:
            self.sem[k] = self.stack.enter_context(self.nc.semaphore('s_%s%d' % (k, self.nsem)))
            self.cnt[k] = 0
        self.nsem += 1

    def _wait(self, e, tok):
        sem, val, src = tok
        if src == e and e == 'pe':
            return
        kn = self.known[e]
        key = id(sem)
        if kn.get(key, 0) >= val:
            return
        kn[key] = val
        self.ops[e].append(('wait', sem, val))

    def _deps(self, e, reads, writes):
        for b in reads:
            if b.w is not None:
                self._wait(e, b.w)
            if b.excl:
                for t in b.r.values():
                    if t[2] != e:
                        self._wait(e, t)
        for b in writes:
            if b.w is not None and b.w[2] != e:
                self._wait(e, b.w)
            for t in b.r.values():
                if t[2] != e:
                    self._wait(e, t)

    def _mark(self, tok, reads, writes):
        key = id(tok[0])
        for b in reads:
            b.r[key] = tok
        for b in writes:
            b.w = tok
            b.r = {}

    def op(self, e, fn, reads=(), writes=()):
        self.nops += 1
        if self.nops > self.maxops:
            return None
        self._deps(e, reads, writes)
        self.cnt[e] += 1
        tok = (self.sem[e], self.cnt[e], e)
        self.ops[e].append(('op', _freeze(fn), self.sem[e]))
        self._mark(tok, reads, writes)
        return tok

    def dma(self, q, out, in_, reads=(), writes=()):
        self.nops += 1
        if self.nops > self.maxops:
            return None
        self._deps(q, reads, writes)
        j = self.dnext[q]
        self.dnext[q] = (j + 1) % self.NDS
        sem = self.dsem[q][j]
        if self.dcnt[q][j] > 0:
            self._wait(q, (sem, self.dcnt[q][j], 'dma'))
        self.dcnt[q][j] += 16
        tok = (sem, self.dcnt[q][j], 'dma')
        self.ops[q].append(('dma', out, in_, sem))
        self._mark(tok, reads, writes)
        return tok

    def barrier(self):
        toks = [(self.sem[k], self.cnt[k], k) for k in self.names if self.cnt[k] > 0]
        for q in self.dsem:
            for j in range(self.NDS):
                if self.dcnt[q][j] > 0:
                    toks.append((self.dsem[q][j], self.dcnt[q][j], 'dma'))
        for e in self.names:
            for t in toks:
                if t[2] != e:
                    self._wait(e, t)
        self._new_sems()

    def emit(self):
        nc = self.nc
        ops = self.ops
        with nc.Block() as block:
            def mk(e):
                def body(engine):
                    pend = []
                    for o in ops[e]:
                        if o[0] == 'wait':
                            pend.append(o)
                            continue
                        for w in pend[:-1]:
                            engine.wait_ge(w[1], w[2])
                        if o[0] == 'op':
                            ins = o[1](engine)
                        else:
                            ins = engine.dma_start(out=o[1], in_=o[2])
                        if pend:
                            ins._wait_ge(pend[-1][1], pend[-1][2])
                        ins.then_inc(o[2] if o[0] == 'op' else o[3], 1 if o[0] == 'op' else 16)
                        pend = []
                    for w in pend:
                        engine.wait_ge(w[1], w[2])
                return body
            block.tensor(mk('pe'))
            block.scalar(mk('act'))
            block.vector(mk('dve'))
            block.gpsimd(mk('pool'))
            block.sync(mk('sp'))


def build(seqs, debug=False, stage=3, maxops=10 ** 9):
    NTOK = sum(seqs)
    NBLK = NTOK // 128
    NTILE = NTOK // T
    nc = bass.Bass("TRN2", target_bir_lowering=False)
    IN = "ExternalInput"
    SCR = "ExternalOutput" if debug else "Internal"

    def din(name, shape, dt=F32):
        return nc.dram_tensor(name, shape, dt, kind=IN).ap()

    def dscr(name, shape, dt):
        return nc.dram_tensor(name, shape, dt, kind=SCR).ap()

    x_d = din("x", [NTOK, D])
    f1w1 = din("f1w1", [D, DFF]); f1w3 = din("f1w3", [D, DFF]); f1w2 = din("f1w2", [DFF, D])
    f2w1 = din("f2w1", [D, DFF]); f2w3 = din("f2w3", [D, DFF]); f2w2 = din("f2w2", [DFF, D])
    win = din("win", [D, 4880]); wpm = din("wpm", [512, D]); wpa = din("wpa", [512, D]); wout = din("wout", [D, D])
    gcol_d = din("gcol", [128, 28])
    vec_d = din("vec", [128, 160])
    cst_d = din("cst", [128, NCST])
    rope_d = din("rope", [128, NBMAX, 64])
    y_d = nc.dram_tensor("y", [NTOK, D], F32, kind="ExternalOutput").ap()

    WS = dscr("WS", [NUNITS, 128, 4096], BF16)
    X1 = dscr("X1", [NTOK, D], F32)
    ZQT = dscr("ZQT", [4, 128, NTOK], BF16)
    ZKT = dscr("ZKT", [4, 128, NTOK], BF16)
    ZK = dscr("ZK", [NTOK, 512], BF16)
    ZV = dscr("ZV", [NTOK, 512], BF16)
    TMO = dscr("TMO", [NTOK, 512], BF16)
    ZAQ = dscr("ZAQ", [NTOK, 512], BF16)
    ZAKV = dscr("ZAKV", [NTOK, 256], BF16)
    GT = dscr("GT", [128, NBLK, 16], F32)
    GMT = dscr("GMT", [8, 128, NTOK], BF16)
    GAT = dscr("GAT", [8, 128, NTOK], BF16)
    HMT = dscr("HMT", [4, 128, NTOK], BF16)
    HAT = dscr("HAT", [4, 128, NTOK], BF16)

    with ExitStack() as top:
        fw = FW(nc, top)
        fw.maxops = maxops
        op = fw.op

        uid = [0]

        def SB(st, name, shape, dt):
            uid[0] += 1
            return st.enter_context(nc.sbuf_tensor("sb%d_%s" % (uid[0], name), shape, dt))

        def PS(st, name, shape, dt):
            uid[0] += 1
            return st.enter_context(nc.psum_tensor("ps%d_%s" % (uid[0], name), shape, dt))

        cst = SB(top, "cst", [128, NCST], F32); b_cst = Buf()
        cstb = SB(top, "cstb", [128, NCST], BF16); b_cstb = Buf()
        gcol = SB(top, "gcol", [128, 28], F32); b_gcol = Buf()
        vecb = SB(top, "vecb", [128, 160], F32); b_vec = Buf()
        cm05 = SB(top, "cm05", [128, 64], F32); b_cm = Buf()
        fw.dma('sp', cst[:], cst_d[:, :], writes=[b_cst])
        fw.dma('sp', gcol[:], gcol_d[:, :], writes=[b_gcol])
        fw.dma('sp', vecb[:], vec_d[:, :], writes=[b_vec])
        op('dve', lambda e: e.tensor_copy(cstb[:], cst[:]), reads=[b_cst], writes=[b_cstb])
        op('dve', lambda e: e.memset(cm05[:], -0.5), writes=[b_cm])
        identb = cstb[:, K_ID:K_ID + 128]

        b_ws = [Buf() for _ in range(NUNITS)]
        cv = {'n': 0}
        udesc = {}
        for ffn_, (w1, w3, w2, ub, ub2, gb) in enumerate([(f1w1, f1w3, f1w2, U_F1, U_F1W2, 0), (f2w1, f2w3, f2w2, U_F2, U_F2W2, 16)]):
            for u in range(11):
                udesc[ub + u] = ([(w1[:, u * 256:(u + 1) * 256], 0, 256), (w3[:, u * 256:(u + 1) * 256], 256, 256)], 8, gb, 1.0, 'dve')
            for u in range(6):
                nj = 4 if u < 5 else 2
                udesc[ub2 + u] = ([(w2[u * 512:u * 512 + nj * 128, :], 0, 1024)], nj, None, 1.0, 'act')
        udesc[U_QT] = ([(win[:, C_MQ:C_MQ + 512], 0, 512)], 8, 8, 1.0, 'dve')
        udesc[U_K] = ([(win[:, C_MK:C_MK + 512], 0, 512)], 8, 8, 1.0, 'dve')
        for i_ in range(2):
            udesc[U_GM + i_] = ([(win[:, C_GM + i_ * 512:C_GM + (i_ + 1) * 512], 0, 512)], 8, 8, 1.0, 'dve')
            udesc[U_GA + i_] = ([(win[:, C_GA + i_ * 512:C_GA + (i_ + 1) * 512], 0, 512)], 8, 8, 1.0, 'dve')
        udesc[U_V] = ([(win[:, C_MV:C_MV + 512], 0, 512)], 8, 8, 1.0, 'dve')
        udesc[U_MO] = ([(win[:, C_MO:C_MO + 512], 0, 512)], 8, 8, 1.0, 'dve')
        udesc[U_AQ] = ([(win[:, C_AQ:C_AQ + 512], 0, 512)], 8, 8, 1.0, 'dve')
        udesc[U_KVG] = ([(win[:, C_AK:C_AK + 256], 0, 256), (win[:, C_MG:C_MG + 16], 256, 16)], 8, 8, 1.0, 'dve')
        udesc[U_PM] = ([(wpm[:, :], 0, 1024)], 4, 24, 0.5, 'dve')
        udesc[U_PA] = ([(wpa[:, :], 0, 1024)], 4, None, 1.0, 'act')
        for c in range(2):
            udesc[U_WO + c] = ([(wout[:, c * 512:(c + 1) * 512], 0, 512)], 8, None, 0.5, 'act')
        units_a = list(range(U_F1, U_F1 + 17)) + [U_QT, U_K, U_GM, U_GM + 1, U_GA, U_GA + 1, U_V, U_MO, U_AQ, U_KVG]
        units_c = [U_PM, U_PA, U_WO, U_WO + 1] + list(range(U_F2, U_F2 + 17))

        def conv_load(u):
            pieces, kcn, gbase, mult, eng = udesc[u]
            i = cv['n'] % 2
            cv['n'] += 1
            width = 512 if kcn == 8 else 1024
            s32 = cv['stg'][i][:, 0:kcn * width].rearrange("p (k c) -> p k c", k=kcn)
            pbufs = []
            for (src, off, w) in pieces:
                pb_ = Buf()
                fw.dma('sp', s32[:, :, off:off + w], src.rearrange("(k p) c -> p k c", p=128), writes=[pb_, cv['b_stg'][i]])
                pbufs.append(pb_)
            return (u, i, pbufs)

        def conv_finish(state):
            u, i, pbufs = state
            pieces, kcn, gbase, mult, eng = udesc[u]
            width = 512 if kcn == 8 else 1024
            s32 = cv['stg'][i][:, 0:kcn * width].rearrange("p (k c) -> p k c", k=kcn)
            s16 = cv['stb'][:, 0:kcn * width].rearrange("p (k c) -> p k c", k=kcn)
            b_stb = cv['b_stb']
            used = max(off + w for (_, off, w) in pieces)
            for kc in range(kcn):
                if gbase is None:
                    if eng == 'dve':
                        op('dve', lambda e, kc=kc: e.tensor_scalar(s16[:, kc, 0:used], s32[:, kc, 0:used], float(mult), None, ALU.mult),
                           reads=pbufs, writes=[b_stb])
                    else:
                        op('act', lambda e, kc=kc: e.activation(s16[:, kc, 0:used], s32[:, kc, 0:used], AF.Copy, scale=float(mult)),
                           reads=pbufs, writes=[b_stb])
                else:
                    gc = gcol[:, gbase + kc:gbase + kc + 1]
                    op('dve', lambda e, kc=kc, gc=gc: e.tensor_scalar(s16[:, kc, 0:used], s32[:, kc, 0:used], gc, float(mult), ALU.mult, ALU.mult),
                       reads=pbufs + [b_gcol], writes=[b_stb])
            fw.dma('pool', WS[u, :, 0:kcn * width], cv['stb'][:, 0:kcn * width], reads=[b_stb, cv['b_stg'][i]], writes=[b_ws[u]])

        def make_ffn_ctx(st, two_xnT=False, nring=3):
            cx = {}
            cx['xres'] = SB(st, "xres", [128, NG, D], F32); cx['b_x'] = [Buf() for _ in range(NG)]
            cx['junk'] = SB(st, "junk", [128, D], BF16); cx['b_junk'] = Buf()
            cx['ss'] = SB(st, "ss", [128, 8], F32); cx['b_ss'] = [Buf() for _ in range(NG)]
            cx['xn'] = [SB(st, "xn%d" % i, [128, D], BF16) for i in range(2)]; cx['b_xn'] = [Buf(), Buf()]
            cx['xnT'] = SB(st, "xnT", [128, 8, T], BF16); cx['b_xnT'] = [Buf() for _ in range(NG)]
            if two_xnT:
                cx['xnT2'] = SB(st, "xnT2", [128, 8, T], BF16); cx['b_xnT2'] = [Buf() for _ in range(NG)]
            cx['hT'] = SB(st, "hT", [128, NJ, T], BF16); cx['b_hT'] = Buf()
            cx['w2'] = SB(st, "w2res", [128, 24, 1024], BF16); cx['b_w2'] = [Buf() for _ in range(6)]
            cx['ring'] = [SB(st, "ring%d" % i, [128, 4096], BF16) for i in range(nring)]; cx['b_ring'] = [Buf() for _ in range(nring)]
            cx['nring'] = nring
            cx['rn'] = 0
            cx['sg'] = [SB(st, "sg%d" % i, [128, T], BF16) for i in range(2)]; cx['b_sg'] = [Buf(), Buf()]
            cx['ptr'] = [PS(st, "ptr%d" % i, [128, 8, 128], BF16) for i in range(2)]; cx['b_ptr'] = [Buf(True), Buf(True)]
            cx['pacc'] = [PS(st, "pacc%d" % i, [128, 512], F32) for i in range(6)]; cx['b_pacc'] = [Buf(True) for _ in range(6)]
            cx['pn'] = 0
            cx['nn'] = 0
            return cx

        def ring_load(cx, u):
            i = cx['rn'] % cx['nring']
            cx['rn'] += 1
            fw.dma('sp', cx['ring'][i][:], WS[u, :, :], reads=[b_ws[u]], writes=[cx['b_ring'][i]])
            return cx['ring'][i], cx['b_ring'][i]

        def next_acc(cx):
            i = cx['pn'] % 6
            cx['pn'] += 1
            return cx['pacc'][i], cx['b_pacc'][i]

        def norm_front(cx, g):
            xres, ss = cx['xres'], cx['ss']
            i = cx['nn'] % 2
            cx['nn'] += 1
            bx, bss = cx['b_x'][g], cx['b_ss'][g]
            op('dve', lambda e: e.memset(ss[:, g:g + 1], 0.0), writes=[bss])
            op('act', lambda e: e.activation(cx['junk'][:], xres[:, g, :], AF.Square, accum_out=ss[:, g:g + 1]),
               reads=[bx, bss], writes=[cx['b_junk'], bss])
            op('dve', lambda e: e.tensor_scalar(ss[:, g:g + 1], ss[:, g:g + 1], float(D * EPS), None, ALU.add), reads=[bss], writes=[bss])
            op('pool', lambda e: e.tensor_tensor(ss[:, g:g + 1], ss[:, g:g + 1], cm05[:, 0:1], ALU.pow), reads=[bss, b_cm], writes=[bss])
            xn = cx['xn'][i]
            op('dve', lambda e: e.tensor_scalar(xn[:], xres[:, g, :], ss[:, g:g + 1], float(math.sqrt(D)), ALU.mult, ALU.mult),
               reads=[bx, bss], writes=[cx['b_xn'][i]])
            return i

        def norm_back(cx, g, i, xnT, b_xnT):
            xn = cx['xn'][i]
            ptr = cx['ptr'][i]
            for k in range(8):
                op('pe', lambda e, k=k: e.transpose(ptr[:, k, :], xn[:, k * 128:(k + 1) * 128], identb),
                   reads=[cx['b_xn'][i], b_cstb], writes=[cx['b_ptr'][i]])
            op('act', lambda e: e.copy(xnT[:, :, g * 128:(g + 1) * 128], ptr[:]), reads=[cx['b_ptr'][i]], writes=[b_xnT[g]])

        def norm_all(cx, xnT, b_xnT):
            pend = None
            for g in range(NG):
                i = norm_front(cx, g)
                if pend is not None:
                    norm_back(cx, pend[0], pend[1], xnT, b_xnT)
                pend = (g, i)
            norm_back(cx, pend[0], pend[1], xnT, b_xnT)

        def ffn(cx, ub, ub2, after_g, xnT, b_xnT):
            hT, w2 = cx['hT'], cx['w2']
            if cx.get('w2_loaded') != ub2:
                cx['w2_loaded'] = ub2
                for u in range(6):
                    nj = 4 if u < 5 else 2
                    fw.dma('pool', w2[:, u * 4:u * 4 + nj, :], WS[ub2 + u, :, 0:nj * 1024].rearrange("p (j c) -> p j c", j=nj),
                           reads=[b_ws[ub2 + u]], writes=[cx['b_w2'][u]])
            for u in range(11):
                rt, rb = ring_load(cx, ub + u)
                rv = rt[:].rearrange("p (k c) -> p k c", k=8)
                for jj in range(2):
                    j = 2 * u + jj
                    A, bA = next_acc(cx)
                    B, bB = next_acc(cx)
                    for kc in range(8):
                        op('pe', lambda e, kc=kc, A=A, jj=jj, rv=rv: e.matmul(A[:], rv[:, kc, jj * 128:(jj + 1) * 128], xnT[:, kc, :], start=(kc == 0), stop=(kc == 7)),
                           reads=[rb] + b_xnT, writes=[bA])
                    for kc in range(8):
                        op('pe', lambda e, kc=kc, B=B, jj=jj, rv=rv: e.matmul(B[:], rv[:, kc, 256 + jj * 128:256 + (jj + 1) * 128], xnT[:, kc, :], start=(kc == 0), stop=(kc == 7)),
                           reads=[rb] + b_xnT, writes=[bB])
                    si = j % 2
                    sg = cx['sg'][si]
                    op('act', lambda e, A=A, sg=sg: e.activation(sg[:], A[:], AF.Silu), reads=[bA], writes=[cx['b_sg'][si]])
                    op('dve', lambda e, B=B, sg=sg, j=j: e.tensor_tensor(hT[:, j, :], sg[:], B[:], ALU.mult),
                       reads=[cx['b_sg'][si], bB], writes=[cx['b_hT']])
            for g in range(NG):
                for c in range(2):
                    P, bP = next_acc(cx)
                    for j in range(NJ):
                        op('pe', lambda e, j=j, P=P, g=g, c=c: e.matmul(P[:], hT[:, j, g * 128:(g + 1) * 128], w2[:, j, c * 512:(c + 1) * 512], start=(j == 0), stop=(j == NJ - 1)),
                           reads=[cx['b_hT'], cx['b_w2'][j // 4]], writes=[bP])
                    xs = cx['xres'][:, g, c * 512:(c + 1) * 512]
                    op('dve', lambda e, P=P, xs=xs: e.scalar_tensor_tensor(xs, P[:], 0.5, xs, ALU.mult, ALU.add),
                       reads=[bP, cx['b_x'][g]], writes=[cx['b_x'][g]])
                after_g(g)

        with ExitStack() as st:
          if stage >= 1:
              cx = make_ffn_ctx(st, two_xnT=True, nring=4)
              xnA, b_xnA = cx['xnT'], cx['b_xnT']
              xnB, b_xnB = cx['xnT2'], cx['b_xnT2']
              cv['stg'] = [SB(st, "stg%d" % i, [128, 4096], F32) for i in range(2)]; cv['b_stg'] = [Buf(), Buf()]
              cv['stb'] = SB(st, "stb", [128, 4096], BF16); cv['b_stb'] = Buf()
              pst = None
              for u_ in units_a:
                  nst = conv_load(u_)
                  if pst is not None:
                      conv_finish(pst)
                  pst = nst
              conv_finish(pst)
              cpend = [None]
              cleft = list(units_c)
              fst = [SB(st, "fst%d" % i, [128, 4, T], BF16) for i in range(2)]; b_fst = [Buf(), Buf()]
              zst = [SB(st, "zst%d" % i, [128, NG, 512], BF16) for i in range(2)]; b_zst = [Buf(), Buf()]
              gst = SB(st, "gst", [128, NG, 16], F32); b_gst = Buf()
              for ti in range(NTILE):
                  t0 = ti * T
                  def load_norm(tn):
                      for g in range(NG):
                          fw.dma('sp', cx['xres'][:, g, :], x_d[tn + g * 128:tn + (g + 1) * 128, :], writes=[cx['b_x'][g]])
                      norm_all(cx, xnA, b_xnA)
                  if ti == 0:
                      load_norm(t0)
                  pend1 = [None]

                  def after1(g, t0=t0):
                      fw.dma('pool', X1[t0 + g * 128:t0 + (g + 1) * 128, :], cx['xres'][:, g, :], reads=[cx['b_x'][g]])
                      i_ = norm_front(cx, g)
                      if pend1[0] is not None:
                          norm_back(cx, pend1[0][0], pend1[0][1], xnB, b_xnB)
                      pend1[0] = (g, i_)
                  ffn(cx, U_F1, U_F1W2, after1, xnA, b_xnA)
                  norm_back(cx, pend1[0][0], pend1[0][1], xnB, b_xnB)
                  if cpend[0] is not None:
                      conv_finish(cpend[0])
                      cpend[0] = None
                  while cleft and (cpend[0] is None):
                      cpend[0] = conv_load(cleft.pop(0))
                      if ti == NTILE - 1 or len(cleft) >= NTILE - 1 - ti:
                          conv_finish(cpend[0])
                          cpend[0] = None
                  if ti + 1 < NTILE:
                      load_norm(t0 + T)
                  xnT = xnB
                  fcount = 0
                  for (u, dst, c0, tanh) in [(U_QT, ZQT, 0, False), (U_GM, GMT, 0, True), (U_GM + 1, GMT, 4, True),
                                             (U_GA, GAT, 0, True), (U_GA + 1, GAT, 4, True), (U_K, ZKT, 0, False)]:
                      rt, rb = ring_load(cx, u)
                      rv = rt[:].rearrange("p (k c) -> p k c", k=8)
                      si = fcount % 2
                      fcount += 1
                      for cc in range(4):
                          P, bP = next_acc(cx)
                          for kc in range(8):
                              op('pe', lambda e, kc=kc, P=P, cc=cc, rv=rv: e.matmul(P[:], rv[:, kc, cc * 128:(cc + 1) * 128], xnT[:, kc, :], start=(kc == 0), stop=(kc == 7)),
                                 reads=[rb] + b_xnB, writes=[bP])
                          if tanh:
                              op('act', lambda e, P=P, cc=cc, si=si: e.activation(fst[si][:, cc, :], P[:], AF.Tanh, scale=0.5), reads=[bP], writes=[b_fst[si]])
                          else:
                              op('dve', lambda e, P=P, cc=cc, si=si: e.tensor_copy(fst[si][:, cc, :], P[:]), reads=[bP], writes=[b_fst[si]])
                      fw.dma('pool', dst[c0:c0 + 4, :, t0:t0 + T].rearrange("h p t -> p h t"), fst[si][:], reads=[b_fst[si]])
                  zc = 0
                  for (u, dst, ncol, kind) in [(U_K, ZK, 512, 'copy'), (U_V, ZV, 512, 'copy'), (U_MO, TMO, 512, 'tanh'),
                                               (U_AQ, ZAQ, 512, 'copy'), (U_KVG, ZAKV, 272, 'kvg')]:
                      if u != U_K:
                          rt, rb = ring_load(cx, u)
                      rv = rt[:].rearrange("p (k c) -> p k c", k=8)
                      si = zc % 2
                      zc += 1
                      for g in range(NG):
                          P, bP = next_acc(cx)
                          for kc in range(8):
                              op('pe', lambda e, kc=kc, P=P, g=g, rv=rv, ncol=ncol: e.matmul(P[:, 0:ncol], xnT[:, kc, g * 128:(g + 1) * 128], rv[:, kc, 0:ncol], start=(kc == 0), stop=(kc == 7)),
                                 reads=[rb, b_xnB[g]], writes=[bP])
                          if kind == 'tanh':
                              op('act', lambda e, P=P, g=g, si=si: e.activation(zst[si][:, g, :], P[:], AF.Tanh, scale=0.5), reads=[bP], writes=[b_zst[si]])
                          elif kind == 'copy':
                              if g % 2 == 0:
                                  op('dve', lambda e, P=P, g=g, si=si: e.tensor_copy(zst[si][:, g, :], P[:]), reads=[bP], writes=[b_zst[si]])
                              else:
                                  op('act', lambda e, P=P, g=g, si=si: e.copy(zst[si][:, g, :], P[:]), reads=[bP], writes=[b_zst[si]])
                          else:
                              op('dve', lambda e, P=P, g=g, si=si: e.tensor_copy(zst[si][:, g, 0:256], P[:, 0:256]), reads=[bP], writes=[b_zst[si]])
                              op('dve', lambda e, P=P, g=g: e.tensor_tensor(gst[:, g, :], P[:, 256:272], vecb[:, 0:16], ALU.add),
                                 reads=[bP, b_vec], writes=[b_gst])
                      if kind == 'kvg':
                          fw.dma('pool', ZAKV[t0:t0 + T, :].rearrange("(g p) c -> p g c", p=128), zst[si][:, :, 0:256], reads=[b_zst[si]])
                          fw.dma('pool', GT[:, t0 // 128:t0 // 128 + NG, :], gst[:], reads=[b_gst])
                      else:
                          fw.dma('pool', dst[t0:t0 + T, :].rearrange("(g p) c -> p g c", p=128), zst[si][:], reads=[b_zst[si]])
              fw.barrier()

        with ExitStack() as st:
          if stage >= 2:
            phase_b(nc, fw, st, SB, PS, seqs, dict(
                  cst=cst, b_cst=b_cst, cstb=cstb, b_cstb=b_cstb, vecb=vecb, b_vec=b_vec, cm05=cm05, b_cm=b_cm,
                ZQT=ZQT, ZKT=ZKT, ZK=ZK, ZV=ZV, TMO=TMO, ZAQ=ZAQ, ZAKV=ZAKV, GT=GT, HMT=HMT, HAT=HAT, rope_d=rope_d))
            fw.barrier()

        with ExitStack() as st:
          if stage >= 3:
              cx = make_ffn_ctx(st)
              hmt = SB(st, "hmt", [128, 4, T], BF16); b_hmt = Buf()
              hat = SB(st, "hat", [128, 4, T], BF16); b_hat = Buf()
              tgm = SB(st, "tgm", [128, 8, T], BF16); b_tgm = Buf()
              tga = SB(st, "tga", [128, 8, T], BF16); b_tga = Buf()
              mT = SB(st, "mT", [128, 8, T], BF16); b_mT = Buf()
              wres = [SB(st, "wres%d" % i, [128, 4096], BF16) for i in range(4)]; b_wres = [Buf() for _ in range(4)]
              m1 = [SB(st, "m1_%d" % i, [128, T], F32) for i in range(2)]; b_m1 = [Buf(), Buf()]
              m2 = [SB(st, "m2_%d" % i, [128, T], F32) for i in range(2)]; b_m2 = [Buf(), Buf()]
              for ti in range(NTILE):
                  t0 = ti * T
                  fw.dma('sp', hmt[:], HMT[:, :, t0:t0 + T].rearrange("h p t -> p h t"), writes=[b_hmt])
                  fw.dma('sp', hat[:], HAT[:, :, t0:t0 + T].rearrange("h p t -> p h t"), writes=[b_hat])
                  fw.dma('sp', tgm[:], GMT[:, :, t0:t0 + T].rearrange("h p t -> p h t"), writes=[b_tgm])
                  fw.dma('sp', tga[:], GAT[:, :, t0:t0 + T].rearrange("h p t -> p h t"), writes=[b_tga])
                  if ti == 0:
                      for (u_, t_, b_) in [(U_PM, wres[0], b_wres[0]), (U_PA, wres[1], b_wres[1]), (U_WO, wres[2], b_wres[2]), (U_WO + 1, wres[3], b_wres[3])]:
                          fw.dma('sp', t_[:], WS[u_, :, :], reads=[b_ws[u_]], writes=[b_])
                  rbm, rba = b_wres[0], b_wres[1]
                  vm = wres[0][:].rearrange("p (k c) -> p k c", k=4)
                  va = wres[1][:].rearrange("p (k c) -> p k c", k=4)
                  for oc in range(8):
                      A, bA = next_acc(cx)
                      B, bB = next_acc(cx)
                      for kc in range(4):
                          op('pe', lambda e, kc=kc, A=A, oc=oc: e.matmul(A[:], vm[:, kc, oc * 128:(oc + 1) * 128], hmt[:, kc, :], start=(kc == 0), stop=(kc == 3)),
                             reads=[rbm, b_hmt], writes=[bA])
                      for kc in range(4):
                          op('pe', lambda e, kc=kc, B=B, oc=oc: e.matmul(B[:], va[:, kc, oc * 128:(oc + 1) * 128], hat[:, kc, :], start=(kc == 0), stop=(kc == 3)),
                             reads=[rba, b_hat], writes=[bB])
                      si = oc % 2
                      op('dve', lambda e, A=A, oc=oc, si=si: e.scalar_tensor_tensor(m1[si][:], tgm[:, oc, :], 1.0, A[:], ALU.add, ALU.mult),
                         reads=[bA, b_tgm], writes=[b_m1[si]])
                      op('dve', lambda e, B=B, oc=oc, si=si: e.scalar_tensor_tensor(m2[si][:], tga[:, oc, :], 1.0, B[:], ALU.add, ALU.mult),
                         reads=[bB, b_tga], writes=[b_m2[si]])
                      op('pool', lambda e, oc=oc, si=si: e.tensor_tensor(mT[:, oc, :], m1[si][:], m2[si][:], ALU.add),
                         reads=[b_m1[si], b_m2[si]], writes=[b_mT])
                  for g in range(NG):
                      fw.dma('sp', cx['xres'][:, g, :], X1[t0 + g * 128:t0 + (g + 1) * 128, :], writes=[cx['b_x'][g]])
                  wo = [(wres[2 + c][:].rearrange("p (k c) -> p k c", k=8), b_wres[2 + c]) for c in range(2)]
                  pend2 = None
                  for g in range(NG):
                      for c in range(2):
                          rv, rb = wo[c]
                          P, bP = next_acc(cx)
                          for kc in range(8):
                              op('pe', lambda e, kc=kc, P=P, g=g, rv=rv: e.matmul(P[:], mT[:, kc, g * 128:(g + 1) * 128], rv[:, kc, :], start=(kc == 0), stop=(kc == 7)),
                                 reads=[rb, b_mT], writes=[bP])
                          xs = cx['xres'][:, g, c * 512:(c + 1) * 512]
                          op('dve', lambda e, P=P, xs=xs: e.tensor_tensor(xs, P[:], xs, ALU.add), reads=[bP, cx['b_x'][g]], writes=[cx['b_x'][g]])
                      i_ = norm_front(cx, g)
                      if pend2 is not None:
                          norm_back(cx, pend2[0], pend2[1], cx['xnT'], cx['b_xnT'])
                      pend2 = (g, i_)
                  norm_back(cx, pend2[0], pend2[1], cx['xnT'], cx['b_xnT'])

                  def after2(g, t0=t0):
                      fw.dma('pool', y_d[t0 + g * 128:t0 + (g + 1) * 128, :], cx['xres'][:, g, :], reads=[cx['b_x'][g]])
                  ffn(cx, U_F2, U_F2W2, after2, cx['xnT'], cx['b_xnT'])
              fw.barrier()
        fw.emit()
    return nc


def phase_b(nc, fw, st, SB, PS, seqs, G):
    op = fw.op
    cst, cstb, vecb, cm05 = G['cst'], G['cstb'], G['vecb'], G['cm05']
    b_cst, b_cstb, b_vec, b_cm = G['b_cst'], G['b_cstb'], G['b_vec'], G['b_cm']
    identb = cstb[:, K_ID:K_ID + 128]
    LNC = float(math.log(128.0 ** -0.5))
    SBK = 4

    gt = SB(st, "gt", [128, NBMAX, 16], F32); b_gt = Buf()
    l1 = SB(st, "l1", [128, NBMAX, 2, 4], F32); b_l1 = Buf()
    ui = SB(st, "ui", [128, NBMAX, 2, 4], F32); b_ui = Buf()
    UU = SB(st, "UU", [128, NBMAX, 2, 4], F32); b_UU = Buf()
    WW = SB(st, "WW", [128, NBMAX, 2, 4], F32); b_WW = Buf()
    EBI = SB(st, "EBI", [128, NBMAX, 2, 4], F32); b_EBI = Buf()
    EG = SB(st, "EG", [128, 2, NBMAX, 2, 4], F32); b_EG = Buf()
    lnc = SB(st, "lnc", [128, 1], F32); b_lnc = Buf()
    op('dve', lambda e: e.memset(lnc[:], LNC), writes=[b_lnc])
    hF = SB(st, "hF", [128, NBMAX, 512], BF16); b_hF = [Buf() for _ in range(NBMAX)]
    qT4 = [SB(st, "qT4_%d" % i, [128, 4, 512], BF16) for i in range(2)]; b_qT4 = [Buf(), Buf()]
    kT4 = [SB(st, "kT4_%d" % i, [128, 4, 512], BF16) for i in range(2)]; b_kT4 = [Buf(), Buf()]
    k4 = [SB(st, "k4_%d" % i, [128, SBK, 512], BF16) for i in range(2)]; b_k4 = [Buf(), Buf()]
    v4 = [SB(st, "v4_%d" % i, [128, SBK, 4, 132], BF16) for i in range(2)]; b_v4 = [[Buf() for _ in range(SBK)] for _ in range(2)]
    tmo4 = [SB(st, "tmo4_%d" % i, [128, SBK, 512], BF16) for i in range(2)]; b_tmo4 = [Buf(), Buf()]
    Cst = SB(st, "Cst", [128, 4, 132], F32); b_CA = Buf()
    Cb = SB(st, "Cb", [128, 4, 132], BF16); b_CbA = Buf()
    vw = [SB(st, "vw%d" % i, [128, 4, 132], BF16) for i in range(2)]; b_vw = [Buf(), Buf()]
    wq = [SB(st, "wq%d" % j, [128, 4, 128], BF16) for j in range(2)]; b_wq = [Buf(), Buf()]
    wtmp = SB(st, "wtmp", [128, 4, 128], F32); b_wtmp = Buf()
    dab = SB(st, "dab", [128, 8], F32); b_dab = Buf()
    hs = SB(st, "hs", [128, 4, 128], F32); b_hs = Buf()
    hsq = SB(st, "hsq", [128, 4, 128], F32); b_hsq = Buf()
    hss = SB(st, "hss", [128, 8], F32); b_hss = Buf()
    hn = SB(st, "hn", [128, 512], F32); b_hn = Buf()
    hmb = SB(st, "hmb", [128, 512], BF16); b_hmb = Buf()
    hst = [SB(st, "hst%d" % i, [128, 4, SBK * 128], BF16) for i in range(2)]; b_hst = [Buf(), Buf()]
    rope = SB(st, "rope", [128, NBMAX, 64], F32); b_rope = Buf()
    kTa = SB(st, "kTa", [64, 2, NBMAX * 128], BF16); b_kTa = [Buf() for _ in range(NBMAX)]
    va = SB(st, "va", [128, NBMAX, 2, 66], BF16); b_va = [Buf() for _ in range(NBMAX)]
    rsk = SB(st, "rsk", [128, NBMAX, 2], F32); b_rsk = [Buf() for _ in range(NBMAX)]
    aq4 = [SB(st, "aq4_%d" % i, [128, SBK, 512], BF16) for i in range(2)]; b_aq4 = [Buf(), Buf()]
    ak4 = [SB(st, "ak4_%d" % i, [128, SBK, 128], BF16) for i in range(2)]; b_ak4 = [Buf(), Buf()]
    kf = SB(st, "kf", [128, SBK, 2, 64], F32); b_kf = Buf()
    ksq = SB(st, "ksq", [128, SBK, 2, 64], F32); b_ksq = Buf()
    kss = SB(st, "kss", [128, SBK, 2], F32); b_kss = Buf()
    kr = SB(st, "kr", [128, SBK, 2, 64], BF16); b_kr = Buf()
    kt1 = SB(st, "kt1", [128, SBK, 2, 32], F32); b_kt1 = Buf()
    kt2 = SB(st, "kt2", [128, SBK, 2, 32], F32); b_kt2 = Buf()
    qf = SB(st, "qf", [128, SBK, 8, 64], F32); b_qf = Buf()
    qsq = SB(st, "qsq", [128, SBK, 8, 64], F32); b_qsq = Buf()
    qss = SB(st, "qss", [128, SBK, 8], F32); b_qss = Buf()
    qr = [SB(st, "qr%d" % i, [128, SBK, 8, 64], BF16) for i in range(2)]; b_qr = [Buf(), Buf()]
    qt1 = qsq[:, 0:2, :, :].rearrange("p a h c -> p (a h c)").rearrange("p (a h c) -> p a h c", a=SBK, h=8); b_qt1 = b_qsq
    qt2 = qsq[:, 2:4, :, :].rearrange("p a h c -> p (a h c)").rearrange("p (a h c) -> p a h c", a=SBK, h=8); b_qt2 = b_qsq
    qTa = SB(st, "qTa", [64, 8, 128], BF16); b_qTa = Buf()
    pp = [SB(st, "pp%d" % i, [128, 512], BF16) for i in range(3)]; b_pp = [Buf() for _ in range(3)]
    esk = SB(st, "esk", [128, 8], F32); b_esk = Buf()
    rden = SB(st, "rden", [128, 8], F32); b_rden = Buf()
    hab = SB(st, "hab", [128, 8, 64], BF16); b_hab = Buf()
    hat_st = [SB(st, "hatst%d" % i, [128, 4, SBK * 128], BF16) for i in range(2)]; b_hatst = [Buf(), Buf()]
    pg = SB(st, "pgs", [128, 1024], F32); b_pg = Buf()
    psc = PS(st, "psc", [128, 4, 128], F32); _b = Buf(True); b_psc = [_b] * 4
    pnd = PS(st, "pnd", [128, 4, 256], F32); _a, _b2 = Buf(True), Buf(True); b_pnd = [_a, _a, _b2, _b2]
    pdc = PS(st, "pdc", [128, 4, 256], F32); _c, _d = Buf(True), Buf(True); b_pdc = [_c, _c, _d, _d]
    ptb = PS(st, "ptb", [128, 8, 128], BF16); b_ptb = Buf(True)
    pdcf = pdc[:].rearrange("p h c -> p (h c)")
    pgp = pdcf; b_pg0 = b_pdc[0]; b_pg1 = b_pdc[2]; b_pgp = [b_pg0, b_pg1]
    asc = PS(st, "asc", [128, 512], F32); b_asc = Buf(True)
    apo = PS(st, "apo", [128, 4, 128], F32); b_apo = Buf(True)
    for i in range(2):
        op('dve', lambda e, i=i: e.memset(v4[i][:], 1.0), writes=b_v4[i])
    op('dve', lambda e: e.memset(va[:], 1.0), writes=b_va)
    fw.dma('sp', rope[:], G['rope_d'][:, :, :], writes=[b_rope])
    op('act', lambda e: e.activation(esk[:], vecb[:, 144:152], AF.Exp), reads=[b_vec], writes=[b_esk])
    qg = vecb[:, 16:80]
    kg = vecb[:, 80:144]

    tok0 = 0
    for S in seqs:
        NB = S // 128
        blk0 = tok0 // 128
        NSB = NB // SBK
        fw.dma('sp', gt[:, 0:NB, :], G['GT'][:, blk0:blk0 + NB, :], writes=[b_gt])
        gtv = gt[:, 0:NB, :].rearrange("p n (t h) -> p n t h", t=4)
        for d in range(2):
            op('act', lambda e, d=d: e.activation(l1[:, 0:NB, d, :], gtv[:, :, 1 + 2 * d, :], AF.Exp, scale=-1.0), reads=[b_gt], writes=[b_l1])
        op('act', lambda e: e.activation(l1[:, 0:NB, :, :], l1[:, 0:NB, :, :], AF.Ln, bias=1.0), reads=[b_l1], writes=[b_l1])
        n4 = NB * 4
        pgv = pg[:, 0:2 * n4].rearrange("p (d n h) -> p d n h", d=2, h=4)
        pgt = pg[:, 256:256 + 2 * n4].rearrange("p (d n h) -> p d n h", d=2, h=4)
        pge = pg[:, 512:512 + 4 * n4].rearrange("p (c d n h) -> p c d n h", c=2, d=2, h=4)
        Pgv = pgp[:, 0:2 * n4].rearrange("p (d n h) -> p d n h", d=2, h=4)
        Pgt = pgp[:, 256:256 + 2 * n4].rearrange("p (d n h) -> p d n h", d=2, h=4)
        Pge = pgp[:, 512:512 + 4 * n4].rearrange("p (c d n h) -> p c d n h", c=2, d=2, h=4)
        for d in range(2):
            msk = cst[:, (K_MF if d == 0 else K_MB):(K_MF if d == 0 else K_MB) + 128]
            op('pe', lambda e, d=d, msk=msk: e.matmul(Pgv[:, d, :, :], msk, l1[:, 0:NB, d, :], start=True, stop=True),
               reads=[b_cst, b_l1], writes=[b_pg0])
            op('pe', lambda e, d=d: e.matmul(Pgt[:, d, :, :], cst[:, K_BLK:K_BLK + 128], l1[:, 0:NB, d, :], start=True, stop=True),
               reads=[b_cst, b_l1], writes=[b_pg0])
            for c in range(2):
                op('pe', lambda e, d=d, c=c: e.matmul(Pge[:, c, d, :, :], cst[:, K_S0 + c * 128:K_S0 + (c + 1) * 128], l1[:, 0:NB, d, :], start=True, stop=True),
                   reads=[b_cst, b_l1], writes=[b_pg1])
        op('act', lambda e: e.copy(pg[:], pgp[:]), reads=b_pgp, writes=[b_pg])
        for d in range(2):
            op('dve', lambda e, d=d: e.tensor_tensor(ui[:, 0:NB, d, :], gtv[:, :, 2 * d, :], pgv[:, d, :, :], ALU.add), reads=[b_gt, b_pg], writes=[b_ui])
            op('act', lambda e, d=d: e.activation(UU[:, 0:NB, d, :], ui[:, 0:NB, d, :], AF.Exp, bias=lnc[:, 0:1]), reads=[b_ui, b_lnc], writes=[b_UU])
            op('act', lambda e, d=d: e.activation(EBI[:, 0:NB, d, :], pgv[:, d, :, :], AF.Exp), reads=[b_pg], writes=[b_EBI])
        for d in range(2):
            op('dve', lambda e, d=d: e.tensor_tensor(ui[:, 0:NB, d, :], ui[:, 0:NB, d, :], pgt[:, d, :, :], ALU.subtract), reads=[b_ui, b_pg, b_UU], writes=[b_ui])
            op('act', lambda e, d=d: e.activation(WW[:, 0:NB, d, :], ui[:, 0:NB, d, :], AF.Exp, bias=lnc[:, 0:1]), reads=[b_ui, b_lnc], writes=[b_WW])
            for c in range(2):
                op('act', lambda e, d=d, c=c: e.activation(EG[:, c, 0:NB, d, :], pge[:, c, d, :, :], AF.Exp, scale=-1.0), reads=[b_pg], writes=[b_EG])

        def gen_m():
            for d in range(2):
                op('dve', lambda e: e.memset(Cst[:], 0.0), writes=[b_CA])
                op('dve', lambda e: e.memset(Cb[:], 0.0), writes=[b_CbA])
                sb_order = list(range(NSB)) if d == 0 else list(range(NSB - 1, -1, -1))
                mask = cst[:, (K_MF if d == 0 else K_MB):(K_MF if d == 0 else K_MB) + 128]
                corder = [0, 1] if d == 0 else [1, 0]

                def loads(sidx):
                    sbi = sb_order[sidx]
                    bi = sidx % 2
                    ts = tok0 + sbi * SBK * 128
                    fw.dma('sp', qT4[bi][:], G['ZQT'][:, :, ts:ts + 512].rearrange("h p t -> p h t"), writes=[b_qT4[bi]])
                    fw.dma('sp', kT4[bi][:], G['ZKT'][:, :, ts:ts + 512].rearrange("h p t -> p h t"), writes=[b_kT4[bi]])
                    fw.dma('sp', k4[bi][:], G['ZK'][ts:ts + 512, :].rearrange("(b p) c -> p b c", p=128), writes=[b_k4[bi]])
                    for b_ in range(SBK):
                        fw.dma('sp', v4[bi][:, b_, :, 0:128], G['ZV'][ts + b_ * 128:ts + (b_ + 1) * 128, :].rearrange("p (h c) -> p h c", h=4), writes=[b_v4[bi][b_]])
                    if d == 1:
                        fw.dma('sp', tmo4[bi][:], G['TMO'][ts:ts + 512, :].rearrange("(b p) c -> p b c", p=128), writes=[b_tmo4[bi]])

                blist = []
                for sidx, sbi in enumerate(sb_order):
                    for bb in (range(SBK) if d == 0 else range(SBK - 1, -1, -1)):
                        blist.append((sidx, sbi, bb))

                def stage_s(i):
                    sidx, sbi, bb = blist[i]
                    bi = sidx % 2
                    nb = sbi * SBK + bb
                    cols = slice(bb * 128, (bb + 1) * 128)
                    vi = i % 2
                    op('pool', lambda e: e.tensor_tensor(vw[vi][:, :, 0:129], v4[bi][:, bb, :, 0:129],
                                                         WW[:, nb, d, :].unsqueeze(2).to_broadcast([128, 4, 129]), ALU.mult),
                       reads=[b_v4[bi][bb], b_WW], writes=[b_vw[vi]])
                    for h in range(4):
                        op('pe', lambda e, h=h: e.matmul(psc[:, h, :], kT4[bi][:, h, cols], qT4[bi][:, h, cols], start=True, stop=True),
                           reads=[b_kT4[bi], b_qT4[bi]], writes=[b_psc[h]])
                    op('dve', lambda e: e.tensor_tensor(wtmp[:], psc[:], UU[:, nb, d, :].unsqueeze(2).to_broadcast([128, 4, 128]), ALU.mult),
                       reads=[b_psc[0], b_UU], writes=[b_wtmp])
                    op('pool', lambda e: e.tensor_tensor(wq[vi][:], wtmp[:], mask.unsqueeze(1).to_broadcast([128, 4, 128]), ALU.mult),
                       reads=[b_wtmp, b_cst], writes=[b_wq[vi]])
                    yield

                def stage_r(i, c):
                    sidx, sbi, bb = blist[i]
                    bi = sidx % 2
                    nb = sbi * SBK + bb
                    vi = i % 2
                    rows = slice(c * 64, (c + 1) * 64)
                    for h in (0, 2, 1, 3):
                        op('pe', lambda e, h=h: e.matmul(pnd[rows, h, 0:129], wq[vi][rows, h, rows], v4[bi][rows, bb, h, 0:129], start=True, stop=False),
                           reads=[b_wq[vi], b_v4[bi][bb]], writes=[b_pnd[h]])
                        op('pe', lambda e, h=h: e.matmul(pnd[rows, h, 0:129], qT4[bi][:, h, bb * 128 + c * 64:bb * 128 + (c + 1) * 64], Cb[:, h, 0:129], start=False, stop=True),
                           reads=[b_qT4[bi], b_CbA], writes=[b_pnd[h]])
                        op('pe', lambda e, h=h: e.matmul(pdc[:, h, 0:129], k4[bi][rows, bb, h * 128:(h + 1) * 128], vw[vi][rows, h, 0:129], start=True, stop=True),
                           reads=[b_k4[bi], b_vw[vi]], writes=[b_pdc[h]])
                    yield
                    op('dve', lambda e: e.tensor_tensor(Cst[:, :, 0:129], Cst[:, :, 0:129], EG[:, c, nb, d, :].unsqueeze(2).to_broadcast([128, 4, 129]), ALU.mult),
                       reads=[b_CA, b_EG], writes=[b_CA])
                    op('dve', lambda e: e.tensor_tensor(Cst[:, :, 0:129], Cst[:, :, 0:129], pdc[:, :, 0:129], ALU.add),
                       reads=[b_CA, b_pdc[0], b_pdc[2]], writes=[b_CA])
                    op('act', lambda e: e.copy(Cb[:, :, 0:129], Cst[:, :, 0:129]), reads=[b_CA], writes=[b_CbA])
                    yield

                def stage_o(i):
                    sidx, sbi, bb = blist[i]
                    bi = sidx % 2
                    nb = sbi * SBK + bb
                    ts = tok0 + sbi * SBK * 128
                    op('act', lambda e: e.activation(dab[:, 0:4], pnd[:, :, 128], AF.Abs), reads=b_pnd, writes=[b_dab])
                    op('dve', lambda e: e.tensor_tensor(dab[:, 0:4], dab[:, 0:4], EBI[:, nb, d, :], ALU.max), reads=[b_dab, b_EBI], writes=[b_dab])
                    op('dve', lambda e: e.reciprocal(dab[:, 0:4], dab[:, 0:4]), reads=[b_dab], writes=[b_dab])
                    yield
                    dabb = dab[:, 0:4].unsqueeze(2).to_broadcast([128, 4, 128])
                    if d == 0:
                        op('dve', lambda e: e.tensor_tensor(hF[:, nb, :].rearrange("p (h c) -> p h c", h=4), pnd[:, :, 0:128], dabb, ALU.mult),
                           reads=b_pnd + [b_dab], writes=[b_hF[nb]])
                        yield
                    else:
                        op('dve', lambda e: e.tensor_tensor(hs[:], pnd[:, :, 0:128], dabb, ALU.mult), reads=b_pnd + [b_dab], writes=[b_hs])
                        op('pool', lambda e: e.tensor_tensor(hs[:], hs[:], hF[:, nb, :].rearrange("p (h c) -> p h c", h=4), ALU.add),
                           reads=[b_hs, b_hF[nb]], writes=[b_hs])
                        yield
                        op('pool', lambda e: e.tensor_tensor(hsq[:], hs[:], hs[:], ALU.mult), reads=[b_hs], writes=[b_hsq])
                        op('dve', lambda e: e.tensor_reduce(hss[:, 0:4], hsq[:], AX.X, ALU.add), reads=[b_hsq], writes=[b_hss])
                        op('dve', lambda e: e.tensor_scalar(hss[:, 0:4], hss[:, 0:4], float(128 * EPS), None, ALU.add), reads=[b_hss], writes=[b_hss])
                        op('pool', lambda e: e.tensor_tensor(hss[:, 0:4], hss[:, 0:4], cm05[:, 0:4], ALU.pow), reads=[b_hss, b_cm], writes=[b_hss])
                        yield
                        op('dve', lambda e: e.tensor_tensor(hn[:].rearrange("p (h c) -> p h c", h=4), hs[:], hss[:, 0:4].unsqueeze(2).to_broadcast([128, 4, 128]), ALU.mult),
                           reads=[b_hs, b_hss], writes=[b_hn])
                        op('dve', lambda e: e.scalar_tensor_tensor(hn[:], tmo4[bi][:, bb, :], 1.0, hn[:], ALU.add, ALU.mult),
                           reads=[b_tmo4[bi], b_hn], writes=[b_hn])
                        op('act', lambda e: e.activation(hmb[:], hn[:], AF.Copy, scale=float(math.sqrt(128.0))), reads=[b_hn], writes=[b_hmb])
                        yield
                        for kc in range(4):
                            op('pe', lambda e, kc=kc: e.transpose(ptb[:, kc, :], hmb[:, kc * 128:(kc + 1) * 128], identb), reads=[b_hmb, b_cstb], writes=[b_ptb])
                        op('act', lambda e: e.copy(hst[bi][:, :, bb * 128:(bb + 1) * 128], ptb[:, 0:4, :]), reads=[b_ptb], writes=[b_hst[bi]])
                        yield
                        if i + 1 == len(blist) or blist[i + 1][0] != sidx:
                            fw.dma('pool', G['HMT'][:, :, ts:ts + 512].rearrange("h p t -> p h t"), hst[bi][:], reads=[b_hst[bi]])

                loads(0)
                yield from stage_s(0)
                for i in range(len(blist)):
                    sidx = blist[i][0]
                    if (i == 0 or blist[i - 1][0] != sidx) and sidx + 1 < NSB:
                        loads(sidx + 1)
                    yield from stage_r(i, corder[0])
                    if i + 1 < len(blist):
                        yield from stage_s(i + 1)
                    yield from stage_r(i, corder[1])
                    yield from stage_o(i)

        def gen_a():
            for sbi in range(NSB):
                bi = sbi % 2
                ts = tok0 + sbi * SBK * 128
                fw.dma('sp', ak4[bi][:], G['ZAKV'][ts:ts + 512, 0:128].rearrange("(b p) c -> p b c", p=128), writes=[b_ak4[bi]])
                for b_ in range(SBK):
                    fw.dma('sp', va[:, sbi * SBK + b_, :, 0:64],
                           G['ZAKV'][ts + b_ * 128:ts + (b_ + 1) * 128, 128:256].rearrange("p (h c) -> p h c", h=2), writes=[b_va[sbi * SBK + b_]])
                kin = ak4[bi][:].rearrange("p b (h c) -> p b h c", h=2)
                nbs = slice(sbi * SBK, (sbi + 1) * SBK)
                op('pool', lambda e: e.tensor_tensor(ksq[:], kin, kin, ALU.mult), reads=[b_ak4[bi]], writes=[b_ksq])
                op('dve', lambda e: e.tensor_reduce(kss[:], ksq[:], AX.X, ALU.add), reads=[b_ksq], writes=[b_kss])
                op('dve', lambda e: e.tensor_scalar(kss[:], kss[:], float(64 * EPS), None, ALU.add), reads=[b_kss], writes=[b_kss])
                op('pool', lambda e: e.tensor_tensor(rsk[:, nbs, :], kss[:], cm05[:, 0:SBK * 2].rearrange("p (b h) -> p b h", h=2), ALU.pow),
                   reads=[b_kss, b_cm], writes=[b_rsk[n_] for n_ in range(sbi * SBK, (sbi + 1) * SBK)])
                yield
                op('pool', lambda e: e.tensor_tensor(kf[:], kin, kg.unsqueeze(1).unsqueeze(1).to_broadcast([128, SBK, 2, 64]), ALU.mult), reads=[b_ak4[bi], b_vec], writes=[b_kf])
                cosb = rope[:, nbs, 0:32].unsqueeze(2).to_broadcast([128, SBK, 2, 32])
                sinb = rope[:, nbs, 32:64].unsqueeze(2).to_broadcast([128, SBK, 2, 32])
                op('pool', lambda e: e.tensor_tensor(kt1[:], kf[:, :, :, 0:32], cosb, ALU.mult), reads=[b_kf, b_rope], writes=[b_kt1])
                op('pool', lambda e: e.tensor_tensor(kt2[:], kf[:, :, :, 32:64], sinb, ALU.mult), reads=[b_kf, b_rope], writes=[b_kt2])
                op('pool', lambda e: e.tensor_tensor(kr[:, :, :, 0:32], kt1[:], kt2[:], ALU.subtract), reads=[b_kt1, b_kt2], writes=[b_kr])
                yield
                op('pool', lambda e: e.tensor_tensor(kt1[:], kf[:, :, :, 32:64], cosb, ALU.mult), reads=[b_kf, b_rope, b_kr], writes=[b_kt1])
                op('pool', lambda e: e.tensor_tensor(kt2[:], kf[:, :, :, 0:32], sinb, ALU.mult), reads=[b_kf, b_rope, b_kr], writes=[b_kt2])
                op('pool', lambda e: e.tensor_tensor(kr[:, :, :, 32:64], kt1[:], kt2[:], ALU.add), reads=[b_kt1, b_kt2], writes=[b_kr])
                yield
                for bb in range(SBK):
                    nb = sbi * SBK + bb
                    for h in range(2):
                        op('pe', lambda e, h=h: e.transpose(ptb[0:64, h, :], kr[:, bb, h, :], identb), reads=[b_kr, b_cstb], writes=[b_ptb])
                    op('act', lambda e: e.copy(kTa[:, :, nb * 128:(nb + 1) * 128], ptb[0:64, 0:2, :]), reads=[b_ptb], writes=[b_kTa[nb]])
                    yield
            def q_front_a(sbi):
                bi = sbi % 2
                ts = tok0 + sbi * SBK * 128
                nbs = slice(sbi * SBK, (sbi + 1) * SBK)
                qro, b_qro = qr[bi], b_qr[bi]
                fw.dma('sp', aq4[bi][:], G['ZAQ'][ts:ts + 512, :].rearrange("(b p) c -> p b c", p=128), writes=[b_aq4[bi]])
                qin = aq4[bi][:].rearrange("p b (h c) -> p b h c", h=8)
                op('pool', lambda e: e.tensor_tensor(qsq[:], qin, qin, ALU.mult), reads=[b_aq4[bi]], writes=[b_qsq])
                op('dve', lambda e: e.tensor_reduce(qss[:], qsq[:], AX.X, ALU.add), reads=[b_qsq], writes=[b_qss])
                op('dve', lambda e: e.tensor_scalar(qss[:], qss[:], float(64 * EPS), None, ALU.add), reads=[b_qss], writes=[b_qss])
                op('pool', lambda e: e.tensor_tensor(qss[:], qss[:], cm05[:, 0:SBK * 8].rearrange("p (b h) -> p b h", h=8), ALU.pow), reads=[b_qss, b_cm], writes=[b_qss])
                yield
                op('dve', lambda e: e.tensor_tensor(qf[:], qin, qss[:].unsqueeze(3).to_broadcast([128, SBK, 8, 64]), ALU.mult), reads=[b_aq4[bi], b_qss], writes=[b_qf])
                op('dve', lambda e: e.scalar_tensor_tensor(qf[:].rearrange("p a h c -> p (a h) c"), qf[:].rearrange("p a h c -> p (a h) c"), 8.0, qg.unsqueeze(1).to_broadcast([128, SBK * 8, 64]), ALU.mult, ALU.mult), reads=[b_qf, b_vec], writes=[b_qf])
                yield
                cosb = rope[:, nbs, 0:32].unsqueeze(2).to_broadcast([128, SBK, 8, 32])
                sinb = rope[:, nbs, 32:64].unsqueeze(2).to_broadcast([128, SBK, 8, 32])
                op('dve', lambda e: e.tensor_tensor(qt1, qf[:, :, :, 0:32], cosb, ALU.mult), reads=[b_qf, b_rope, b_qss], writes=[b_qt1])
                op('pool', lambda e: e.tensor_tensor(qt2, qf[:, :, :, 32:64], sinb, ALU.mult), reads=[b_qf, b_rope, b_qss], writes=[b_qt2])
                op('dve', lambda e: e.tensor_tensor(qro[:, :, :, 0:32], qt1, qt2, ALU.subtract), reads=[b_qt1, b_qt2], writes=[b_qro])
                yield
                op('dve', lambda e: e.tensor_tensor(qt1, qf[:, :, :, 32:64], cosb, ALU.mult), reads=[b_qf, b_rope, b_qro], writes=[b_qt1])
                op('pool', lambda e: e.tensor_tensor(qt2, qf[:, :, :, 0:32], sinb, ALU.mult), reads=[b_qf, b_rope, b_qro], writes=[b_qt2])
                op('dve', lambda e: e.tensor_tensor(qro[:, :, :, 32:64], qt1, qt2, ALU.add), reads=[b_qt1, b_qt2], writes=[b_qro])
                yield

            def q_front_b(nb):
                sbi, bb = nb // SBK, nb % SBK
                bi = sbi % 2
                for h in range(8):
                    op('pe', lambda e, h=h: e.transpose(ptb[0:64, h, :], qr[bi][:, bb, h, :], identb), reads=[b_qr[bi], b_cstb], writes=[b_ptb])
                op('act', lambda e: e.copy(qTa[:], ptb[0:64, :, :]), reads=[b_ptb], writes=[b_qTa])
                yield

            def q_back(nb):
                sbi, bb = nb // SBK, nb % SBK
                bi = sbi % 2
                ts = tok0 + sbi * SBK * 128
                kbs = [kb for kb in (nb - 1, nb, nb + 1) if 0 <= kb < NB]
                for kvh in range(2):
                    for ki, kb in enumerate(kbs):
                        op('pe', lambda e, kb=kb: e.matmul(asc[:], kTa[:, kvh, kb * 128:(kb + 1) * 128],
                                                           qTa[:, kvh * 4:(kvh + 1) * 4, :], start=True, stop=True),
                           reads=[b_kTa[kb], b_qTa], writes=[b_asc])
                        op('act', lambda e, ki=ki, kb=kb: e.activation(pp[ki][:], asc[:], AF.Exp, scale=rsk[:, kb, kvh:kvh + 1]),
                           reads=[b_asc, b_rsk[kb]], writes=[b_pp[ki]])
                        yield
                        if kb != nb:
                            mk = cstb[:, (K_GE if kb < nb else K_LE):(K_GE if kb < nb else K_LE) + 128]
                            op('pool', lambda e, ki=ki, mk=mk: e.tensor_tensor(pp[ki][:].rearrange("p (g q) -> p g q", g=4), pp[ki][:].rearrange("p (g q) -> p g q", g=4),
                                                                              mk.unsqueeze(1).to_broadcast([128, 4, 128]), ALU.mult),
                               reads=[b_pp[ki], b_cstb], writes=[b_pp[ki]])
                    for g in range(4):
                        for ki, kb in enumerate(kbs):
                            op('pe', lambda e, g=g, ki=ki, kb=kb: e.matmul(apo[:, g, 0:65], pp[ki][:, g * 128:(g + 1) * 128], va[:, kb, kvh, 0:65],
                                                                          start=(ki == 0), stop=(ki == len(kbs) - 1)),
                               reads=[b_pp[ki], b_va[kb]], writes=[b_apo])
                    yield
                    hsl = slice(kvh * 4, (kvh + 1) * 4)
                    op('dve', lambda e: e.tensor_tensor(rden[:, hsl], apo[:, :, 64], esk[:, hsl], ALU.add), reads=[b_apo, b_esk], writes=[b_rden])
                    op('dve', lambda e: e.reciprocal(rden[:, hsl], rden[:, hsl]), reads=[b_rden], writes=[b_rden])
                    op('dve', lambda e: e.tensor_tensor(hab[:, hsl, :], apo[:, :, 0:64], rden[:, hsl].unsqueeze(2).to_broadcast([128, 4, 64]), ALU.mult),
                       reads=[b_apo, b_rden], writes=[b_hab])
                    yield
                habf = hab[:].rearrange("p h c -> p (h c)")
                for kc in range(4):
                    op('pe', lambda e, kc=kc: e.transpose(ptb[:, kc, :], habf[:, kc * 128:(kc + 1) * 128], identb), reads=[b_hab, b_cstb], writes=[b_ptb])
                op('act', lambda e: e.copy(hat_st[bi][:, :, bb * 128:(bb + 1) * 128], ptb[:, 0:4, :]), reads=[b_ptb], writes=[b_hatst[bi]])
                yield
                if bb == SBK - 1:
                    fw.dma('pool', G['HAT'][:, :, ts:ts + 512].rearrange("h p t -> p h t"), hat_st[bi][:], reads=[b_hatst[bi]])

            yield from q_front_a(0)
            yield from q_front_b(0)
            for nb in range(NB):
                if nb % SBK == 0 and nb // SBK + 1 < NSB:
                    yield from q_front_a(nb // SBK + 1)
                yield from q_back(nb)
                if nb + 1 < NB:
                    yield from q_front_b(nb + 1)

        gens = [gen_m(), gen_a()]
        while gens:
            for g_ in list(gens):
                try:
                    next(g_)
                except StopIteration:
                    gens.remove(g_)
        tok0 += S


def make_consts():
    c = np.zeros((128, NCST), np.float32)
    i = np.arange(128)
    s, t = i[:, None], i[None, :]
    same = (s // 64) == (t // 64)
    c[:, K_ID:K_ID + 128] = np.eye(128)
    c[:, K_LE:K_LE + 128] = (s <= t)
    c[:, K_GE:K_GE + 128] = (s >= t)
    c[:, K_BLK:K_BLK + 128] = same
    c[:, K_MF:K_MF + 128] = same & (s <= t)
    c[:, K_MB:K_MB + 128] = same & (s >= t)
    c[:, K_ONE:K_ONE + 128] = 1.0
    c[0:64, K_S0:K_S0 + 128] = 1.0
    c[64:128, K_S1:K_S1 + 128] = 1.0
    return c


def make_rope():
    half = 32
    inv = np.power(np.float32(10000.0), -np.arange(half, dtype=np.float32) / np.float32(half)).astype(np.float32)
    pos = np.arange(NBMAX * 128, dtype=np.float32)
    ang = (pos[:, None] * inv[None, :]).astype(np.float32)
    r = np.concatenate([np.cos(ang), np.sin(ang)], axis=-1).astype(np.float32)
    return np.ascontiguousarray(r.reshape(NBMAX, 128, 64).transpose(1, 0, 2))


def common_inputs(inp):
    f = lambda k: np.ascontiguousarray(np.asarray(inp[k], np.float32)[0])
    gcol = np.concatenate([f('ffn1_norm').reshape(8, 128).T, f('mix_norm').reshape(8, 128).T,
                           f('ffn2_norm').reshape(8, 128).T, f('m_norm').reshape(4, 128).T], axis=1)
    vec = np.zeros((128, 160), np.float32)
    vec[:, 0:16] = f('b_gates'); vec[:, 16:80] = f('q_norm'); vec[:, 80:144] = f('k_norm'); vec[:, 144:152] = f('sink')
    return {
        "f1w1": f('ffn1_w1'), "f1w3": f('ffn1_w3'), "f1w2": f('ffn1_w2'),
        "f2w1": f('ffn2_w1'), "f2w3": f('ffn2_w3'), "f2w2": f('ffn2_w2'),
        "win": f('w_in'), "wpm": f('w_pm'), "wpa": f('w_pa'), "wout": f('w_out'),
        "gcol": np.ascontiguousarray(gcol), "vec": vec, "cst": make_consts(), "rope": make_rope(),
    }


def kernel(**inputs):
    xp = np.asarray(inputs['x_prompt'], np.float32)
    xs = np.asarray(inputs['x_sample'], np.float32)
    seqs = [SEQ] * PROMPT_PER_CORE + [DEC_SEQ]
    nc = build(seqs)
    com = common_inputs(inputs)
    in_maps = []
    for c in range(N_CORES):
        xc = np.concatenate([xp[c * PROMPT_PER_CORE:(c + 1) * PROMPT_PER_CORE].reshape(-1, D), xs[c]], axis=0)
        m = dict(com)
        m["x"] = np.ascontiguousarray(xc)
        in_maps.append(m)
    res = run_bass_kernel_spmd(nc, in_maps, core_ids=list(range(N_CORES)))
    yp = np.empty_like(xp)
    ys = np.empty_like(xs)
    npt = PROMPT_PER_CORE * SEQ
    for c in range(N_CORES):
        y = res.results[c]["y"]
        yp[c * PROMPT_PER_CORE:(c + 1) * PROMPT_PER_CORE] = y[:npt].reshape(PROMPT_PER_CORE, SEQ, D)
        ys[c] = y[npt:]
    return (yp, ys)
```
